# Optimizing a Trainium2 kernel written in Bass

```python
import math
import jax, jax.numpy as jnp
from jax import lax
import numpy as np

D_MODEL = 1024
BATCH = 16
SEQ = 2048
DEPTH = 4

GRID_W = 64
CTX_LEN = 256
HEAD_DIM = 64
ATT_W = D_MODEL // 2
NA_HEADS = ATT_W // HEAD_DIM
NA_ROWS_MAX = 8
NA_COLS = 16
NA_QCOLS = 16
SG_W = D_MODEL // 4
SG_GROUPS = 4
SG_CHUNK = 128
CV_W = D_MODEL // 4
CV_GROUPS = 4
CV_KERNEL = 31
MIX_W = ATT_W + SG_W + CV_W
N_IN = 3 * ATT_W + 2 * SG_W + 2 * CV_W
D_FF = 256 * ((8 * D_MODEL // 3 + 255) // 256)
FFN_KERNEL = 3
EPS = 1e-6
NEG_INF = -1e30

kernel_name = "hybrid_natten_gmlp_conformer_dit"


def _rmsnorm(x, g):
    x32 = x.astype(jnp.float32)
    y = x32 * lax.rsqrt(jnp.mean(x32 * x32, axis=-1, keepdims=True) + EPS)
    return y.astype(x.dtype) * g


def _layernorm(x, g, b):
    x32 = x.astype(jnp.float32)
    mu = jnp.mean(x32, axis=-1, keepdims=True)
    var = jnp.mean(jnp.square(x32 - mu), axis=-1, keepdims=True)
    return ((x32 - mu) * lax.rsqrt(var + EPS)).astype(x.dtype) * g + b


def _group_layernorm(x, g, b, groups):
    shp = x.shape
    x32 = x.astype(jnp.float32).reshape(shp[:-1] + (groups, shp[-1] // groups))
    mu = jnp.mean(x32, axis=-1, keepdims=True)
    var = jnp.mean(jnp.square(x32 - mu), axis=-1, keepdims=True)
    y = ((x32 - mu) * lax.rsqrt(var + EPS)).reshape(shp).astype(x.dtype)
    return y * g + b


def _modulate(h, shift, scale):
    return h * (1 + scale) + shift


def _dwconv(x, w, b):
    k = w.shape[0]
    ch = x.shape[-1]
    y = lax.conv_general_dilated(
        x, w[:, None, :].astype(x.dtype), window_strides=(1,),
        padding=[((k - 1) // 2, k // 2)],
        dimension_numbers=("NWC", "WIO", "NWC"), feature_group_count=ch)
    return y + b


def _heads(t):
    bn, ln, _ = t.shape
    return t.reshape(bn, ln, NA_HEADS, HEAD_DIM).transpose(0, 2, 1, 3)


def _axis_span(length, win, qblk):
    span = min(qblk + win, length)
    start = np.clip(np.arange(length) - win // 2, 0, length - win)
    a = np.clip(np.arange(0, length, qblk) - win // 2, 0, length - span)
    qpos = np.arange(length).reshape(-1, qblk)
    kpos = a[:, None] + np.arange(span)[None, :]
    s_q = start[qpos][:, :, None]
    inwin = (kpos[:, None, :] >= s_q) & (kpos[:, None, :] < s_q + win)
    off = kpos[:, None, :] - qpos[:, :, None]
    return kpos, inwin, off


def _na_index(rows):
    win_r = min(NA_ROWS_MAX, rows)
    qr = next(q for q in (8, 4, 2, 1) if rows % q == 0)
    kr, mr, dr = _axis_span(rows, win_r, qr)
    kc, mc, dc = _axis_span(GRID_W, NA_COLS, NA_QCOLS)
    nbr, nbc = kr.shape[0], kc.shape[0]
    nq = qr * NA_QCOLS
    key_idx = (kr[:, None, :, None] * GRID_W + kc[None, :, None, :]).reshape(nbr * nbc, -1)
    mask = (mr[:, None, :, None, :, None] & mc[None, :, None, :, None, :]).reshape(nbr * nbc, nq, -1)
    ridx = np.clip(dr + NA_ROWS_MAX - 1, 0, 2 * NA_ROWS_MAX - 2)
    cidx = np.clip(dc + NA_COLS - 1, 0, 2 * NA_COLS - 2)
    bias_idx = (ridx[:, None, :, None, :, None] * (2 * NA_COLS - 1)
                + cidx[None, :, None, :, None, :]).reshape(nbr * nbc, nq, -1)
    return qr, key_idx.astype(np.int32), mask, bias_idx.astype(np.int32)


def _neighbourhood_attention(q, k, v, k_ctx, v_ctx, rpb, rows):
    bn, nh, sl, dh = q.shape
    qr, key_idx, mask, bias_idx = _na_index(rows)
    nbr, nbc = rows // qr, GRID_W // NA_QCOLS
    span = key_idx.shape[-1]
    qb = q.reshape(bn, nh, nbr, qr, nbc, NA_QCOLS, dh).transpose(2, 4, 0, 1, 3, 5, 6)
    qb = qb.reshape(nbr * nbc, bn, nh, qr * NA_QCOLS, dh)
    rpb_flat = rpb.reshape(nh, -1)
    scale = dh ** -0.5

    def block(args):
        qblk, kidx, m, bidx = args
        kb = jnp.take(k, kidx, axis=2)
        vb = jnp.take(v, kidx, axis=2)
        s_loc = jnp.einsum("bhqd,bhkd->bhqk", qblk, kb).astype(jnp.float32) * scale
        s_loc = s_loc + jnp.take(rpb_flat, bidx, axis=1).astype(jnp.float32)[None]
        s_loc = jnp.where(m[None, None], s_loc, NEG_INF)
        s_ctx = jnp.einsum("bhqd,bhkd->bhqk", qblk, k_ctx).astype(jnp.float32) * scale
        p = jax.nn.softmax(jnp.concatenate([s_loc, s_ctx], axis=-1), axis=-1).astype(v.dtype)
        return (jnp.einsum("bhqk,bhkd->bhqd", p[..., :span], vb)
                + jnp.einsum("bhqk,bhkd->bhqd", p[..., span:], v_ctx))

    out = lax.map(block, (qb, jnp.asarray(key_idx), jnp.asarray(mask), jnp.asarray(bias_idx)))
    out = out.reshape(nbr, nbc, bn, nh, qr, NA_QCOLS, dh).transpose(2, 0, 4, 1, 5, 3, 6)
    return out.reshape(bn, sl, nh * dh)


def _context_attention(q, k, v):
    s = jnp.einsum("bhqd,bhkd->bhqk", q, k).astype(jnp.float32) * (q.shape[-1] ** -0.5)
    p = jax.nn.softmax(s, axis=-1).astype(v.dtype)
    o = jnp.einsum("bhqk,bhkd->bhqd", p, v)
    bn, nh, ln, dh = o.shape
    return o.transpose(0, 2, 1, 3).reshape(bn, ln, nh * dh)


def _spatial_gating(u, v, ln_g, ln_b, w_s, b_s):
    u = jax.nn.gelu(u)
    v = _layernorm(jax.nn.gelu(v), ln_g, ln_b)
    bn, ln, ch = v.shape
    vr = v.reshape(bn, ln // SG_CHUNK, SG_CHUNK, SG_GROUPS, ch // SG_GROUPS)
    mixed = jnp.einsum("gpq,bnqgc->bnpgc", w_s, vr) + b_s.T[None, None, :, :, None]
    return u * mixed.reshape(bn, ln, ch)


def _conformer_conv(a, gate, w, b, ng, nb):
    h = a * jax.nn.sigmoid(gate)
    h = _dwconv(h, w, b)
    h = _group_layernorm(h, ng, nb, CV_GROUPS)
    return jax.nn.silu(h)


def _mix_out(att, z, sg_ng, sg_nb, sg_w, sg_b, cv_w, cv_b, cv_ng, cv_nb, w_out):
    o1 = 3 * ATT_W
    o2 = o1 + 2 * SG_W
    sg = _spatial_gating(z[..., o1:o1 + SG_W], z[..., o1 + SG_W:o2], sg_ng, sg_nb, sg_w, sg_b)
    cv = _conformer_conv(z[..., o2:o2 + CV_W], z[..., o2 + CV_W:o2 + 2 * CV_W],
                         cv_w, cv_b, cv_ng, cv_nb)
    return jnp.concatenate([att, sg, cv], axis=-1) @ w_out


def _conv_ffn(h, w_up, conv_w, conv_b, w_down):
    gate, up = jnp.split(h @ w_up, 2, axis=-1)
    gate = _dwconv(gate, conv_w, conv_b)
    return (jax.nn.silu(gate) * up) @ w_down


def setup_inputs(seed: int = 0) -> dict:
    key = jax.random.key(seed)
    ks = jax.random.split(key, 24)
    D = D_MODEL

    def n(k, shape, s):
        return jax.random.normal(k, shape, jnp.float32) * s

    return {
        "x": n(ks[0], (BATCH, SEQ, D), 1.0),
        "c": n(ks[1], (BATCH, D), 1.0),
        "ctx": n(ks[2], (BATCH, CTX_LEN, D), 1.0),
        "c_ctx": n(ks[3], (D,), 1.0),
        "ada_w": n(ks[4], (DEPTH, D, 6 * D), 0.5 * D ** -0.5),
        "ada_b": n(ks[5], (DEPTH, 6 * D), 0.02),
        "norm1_g": 1.0 + n(ks[6], (DEPTH, D), 0.05),
        "w_in": n(ks[7], (DEPTH, D, N_IN), D ** -0.5),
        "na_rpb": n(ks[8], (DEPTH, NA_HEADS, 2 * NA_ROWS_MAX - 1, 2 * NA_COLS - 1), 0.5),
        "sg_norm_g": 1.0 + n(ks[9], (DEPTH, SG_W), 0.05),
        "sg_norm_b": n(ks[10], (DEPTH, SG_W), 0.02),
        "sg_w": n(ks[11], (DEPTH, SG_GROUPS, SG_CHUNK, SG_CHUNK), SG_CHUNK ** -0.5),
        "sg_b": 1.0 + n(ks[12], (DEPTH, SG_GROUPS, SG_CHUNK), 0.1),
        "cv_w": n(ks[13], (DEPTH, CV_KERNEL, CV_W), CV_KERNEL ** -0.5),
        "cv_b": n(ks[14], (DEPTH, CV_W), 0.02),
        "cv_norm_g": 1.0 + n(ks[15], (DEPTH, CV_W), 0.05),
        "cv_norm_b": n(ks[16], (DEPTH, CV_W), 0.02),
        "w_out": n(ks[17], (DEPTH, MIX_W, D), MIX_W ** -0.5),
        "norm2_g": 1.0 + n(ks[18], (DEPTH, D), 0.05),
        "ffn_w_up": n(ks[19], (DEPTH, D, 2 * D_FF), D ** -0.5),
        "ffn_conv_w": n(ks[20], (DEPTH, FFN_KERNEL, D_FF), FFN_KERNEL ** -0.5),
        "ffn_conv_b": n(ks[21], (DEPTH, D_FF), 0.02),
        "ffn_w_down": n(ks[22], (DEPTH, D_FF, D), D_FF ** -0.5),
        "final_norm_g": 1.0 + n(ks[23], (D,), 0.05),
    }


def reference(x, c, ctx, c_ctx, ada_w, ada_b, norm1_g, w_in, na_rpb, sg_norm_g, sg_norm_b,
              sg_w, sg_b, cv_w, cv_b, cv_norm_g, cv_norm_b, w_out, norm2_g, ffn_w_up,
              ffn_conv_w, ffn_conv_b, ffn_w_down, final_norm_g):
    rows = x.shape[1] // GRID_W
    xc = ctx
    for l in range(DEPTH):
        last = l == DEPTH - 1
        mod_l = jnp.split((jax.nn.silu(c) @ ada_w[l] + ada_b[l])[:, None, :], 6, axis=-1)
        mod_c = jnp.split((jax.nn.silu(c_ctx) @ ada_w[l] + ada_b[l])[None, None, :], 6, axis=-1)

        h = _modulate(_rmsnorm(x, norm1_g[l]), mod_l[0], mod_l[1])
        hc = _modulate(_rmsnorm(xc, norm1_g[l]), mod_c[0], mod_c[1])
        z = h @ w_in[l]
        q = _heads(z[..., :ATT_W])
        k = _heads(z[..., ATT_W:2 * ATT_W])
        v = _heads(z[..., 2 * ATT_W:3 * ATT_W])
        if last:
            kvc = hc @ w_in[l][:, ATT_W:3 * ATT_W]
            kc = _heads(kvc[..., :ATT_W])
            vc = _heads(kvc[..., ATT_W:])
        else:
            zc = hc @ w_in[l]
            kc = _heads(zc[..., ATT_W:2 * ATT_W])
            vc = _heads(zc[..., 2 * ATT_W:3 * ATT_W])
        att = _neighbourhood_attention(q, k, v, kc, vc, na_rpb[l], rows)
        y = _mix_out(att, z, sg_norm_g[l], sg_norm_b[l], sg_w[l], sg_b[l], cv_w[l], cv_b[l],
                     cv_norm_g[l], cv_norm_b[l], w_out[l])
        x = x + mod_l[2] * y
        if not last:
            att_c = _context_attention(_heads(zc[..., :ATT_W]), kc, vc)
            yc = _mix_out(att_c, zc, sg_norm_g[l], sg_norm_b[l], sg_w[l], sg_b[l], cv_w[l],
                          cv_b[l], cv_norm_g[l], cv_norm_b[l], w_out[l])
            xc = xc + mod_c[2] * yc

        h2 = _modulate(_rmsnorm(x, norm2_g[l]), mod_l[3], mod_l[4])
        x = x + mod_l[5] * _conv_ffn(h2, ffn_w_up[l], ffn_conv_w[l], ffn_conv_b[l], ffn_w_down[l])
        if not last:
            h2c = _modulate(_rmsnorm(xc, norm2_g[l]), mod_c[3], mod_c[4])
            xc = xc + mod_c[5] * _conv_ffn(h2c, ffn_w_up[l], ffn_conv_w[l], ffn_conv_b[l],
                                           ffn_w_down[l])
    return _rmsnorm(x, final_norm_g)
```

```python
import numpy as np
from bisect import bisect_left
from contextlib import ExitStack
import concourse.bass as bass
import concourse.mybir as mybir
from concourse.bass_utils import run_bass_kernel_spmd

F32 = mybir.dt.float32
BF16 = mybir.dt.bfloat16
AF = mybir.ActivationFunctionType
ALU = mybir.AluOpType

D = 1024
SEQ = 2048
CTXL = 256
NL = 4
TOK = SEQ + CTXL
NT = TOK // 128
NF = 22
FG = 6
EPS = 1e-6
SB_BASE = 16512
SB_END = 229344
AR_ROW = [0, 4, 12, 16]
AC_COL = [0, 8, 24, 32]
PAT = [0, 1, 1, 2]
GROUPS = [(0, 512), (512, 512), (1024, 512), (1536, 512), (2048, 256)]


class TT:
    __slots__ = ("w", "r")

    def __init__(self):
        self.w = None
        self.r = {}


class Eng:
    def __init__(self, name, obj, sems, lim=4000):
        self.name = name
        self.obj = obj
        self.sems = sems
        self.si = 0
        self.cnt = 0
        self.lim = lim
        self.ins = []
        self.incidx = []
        self.inctok = []
        self.waited = {}
        self.dsems = []
        self.dcount = 0

    def add(self, ins, inc):
        self.ins.append(ins)
        i = len(self.ins) - 1
        if inc:
            self._inc(i)
        return i

    def _inc(self, i):
        if self.cnt >= self.lim:
            self.si += 1
            self.cnt = 0
        sem = self.sems[self.si]
        self.cnt += 1
        self.ins[i].then_inc(sem, 1)
        self.incidx.append(i)
        self.inctok.append((sem, self.cnt))

    def resolve(self, i):
        j = bisect_left(self.incidx, i)
        if j == len(self.incidx):
            self._inc(len(self.ins) - 1)
        return self.inctok[j]


def need(E, ref):
    if ref[0] == "e":
        sem, val = ref[1].resolve(ref[2])
    else:
        sem, val = ref[1], ref[2]
    k = sem.num
    if E.waited.get(k, 0) >= val:
        return
    E.obj.wait_ge(sem, val)
    E.waited[k] = val


def _deps(E, R, W):
    for t in R:
        if t.w is not None:
            need(E, t.w)
    for t in W:
        if t.w is not None and not (t.w[0] == "e" and t.w[1] is E):
            need(E, t.w)
        for rn, rt in t.r.items():
            if rn != E.name:
                need(E, rt)


def op(E, fn, R=(), W=(), inc=True):
    _deps(E, R, W)
    ins = fn()
    idx = E.add(ins, inc)
    ref = ("e", E, idx)
    for t in R:
        t.r[E.name] = ref
    for t in W:
        t.w = ref
        t.r = {}
    return ref


_dma_uid = [0]


def dma(Q, out, in_, R=(), W=()):
    k = Q.dcount
    ns = len(Q.dsems)
    sem = Q.dsems[k % ns]
    val = 16 * (k // ns + 1)
    if k >= ns:
        need(Q, ("d", sem, val - 16))
    _deps(Q, R, W)
    Q.obj.dma_start(out=out, in_=in_).then_inc(sem, 16)
    Q.dcount += 1
    ref = ("d", sem, val)
    _dma_uid[0] += 1
    for t in R:
        t.r["dma%d" % _dma_uid[0]] = ref
    for t in W:
        t.w = ref
        t.r = {}
    return ref


def barrier(engs):
    refs = []
    for F in engs:
        if F.ins:
            refs.append(("e", F, len(F.ins) - 1))
        if F.dcount:
            k = F.dcount
            ns = len(F.dsems)
            for kk in range(max(0, k - ns), k):
                refs.append(("d", F.dsems[kk % ns], 16 * (kk // ns + 1)))
    for E in engs:
        for r in refs:
            need(E, r)


def build(nl=NL, nseq=2):
    nc = bass.Bass("TRN2", target_bir_lowering=False)

    def din(name, shape):
        return nc.dram_tensor(name, list(shape), F32, kind="ExternalInput").ap()

    x2 = din("x2", [2, SEQ, D])
    ctx2 = din("ctx2", [2, CTXL, D])
    cT = din("cT", [128, 8, 3])
    adaw = din("adaw", [nl, 12, 128, 4096])
    adab = din("adab", [nl, 6144])
    n1g = din("n1g", [nl, D])
    n2g = din("n2g", [nl, D])
    fng = din("fng", [D])
    winA = din("winA", [nl, 4, 128, 3072])
    winB = din("winB", [nl, 2, 128, 4096])
    btab = din("btab", [nl, 8, 128, 4608])
    sglng = din("sglng", [nl, 256])
    sglnb = din("sglnb", [nl, 256])
    sgwT = din("sgwT", [nl, 128, 512])
    sgbc = din("sgbc", [nl, 128, 4])
    cvwc = din("cvwc", [nl, 128, 62])
    cvpc = din("cvpc", [nl, 128, 6])
    woutd = din("woutd", [nl, 2, 128, 4096])
    wffn = din("wffn", [nl, NF, 128, 3072])
    fcwc = din("fcwc", [nl, 128, 66])
    fcbc = din("fcbc", [nl, 128, NF])
    y2 = nc.dram_tensor("y2", [2, SEQ, D], F32, kind="ExternalOutput").ap()
    grow = nc.dram_tensor("grow", [nl, 6, 3, 1024], F32).ap()

    off = [SB_BASE]
    uid = [0]

    def sb(shape, dt, at=None):
        nb = int(np.prod(shape[1:])) * (4 if dt == F32 else 2)
        o = off[0] if at is None else at
        o = (o + 31) // 32 * 32
        uid[0] += 1
        h = nc.alloc_sbuf_tensor_at("t%d" % uid[0], list(shape), dt, offset=o)
        if at is None:
            off[0] = o + nb
        return h, o + nb

    X, _ = sb([128, NT, D], F32)
    HT, _ = sb([128, 8, TOK], BF16)
    ATT_BASE = off[0]
    ATT, _ = sb([128, NT, 768], BF16)
    RINGS = [sb([128, 4608], BF16)[0] for _ in range(3)]
    ident, _ = sb([128, 128], BF16)
    identf, _ = sb([128, 128], F32)
    blk64, _ = sb([128, 128], F32)
    ones_bf, _ = sb([128, 128], BF16)
    epsc, _ = sb([128, 1], F32)
    SC, _ = sb([128, 8, 3], BF16)
    SCf, _ = sb([128, 8, 3], F32)
    MODC, _ = sb([128, NL, 32, 3], F32)
    ADABC, _ = sb([128, 48], F32)
    G1C, _ = sb([128, 8], F32)
    G2C, _ = sb([128, 8], F32)
    AB, _ = sb([128, 2, 4, 8], F32)
    SS, _ = sb([128, 2, NT], F32)
    SGW, _ = sb([128, 512], BF16)
    SGBC, _ = sb([128, 4], F32)
    CVWC, _ = sb([128, 62], F32)
    CVPC, _ = sb([128, 6], F32)
    FCWC, _ = sb([128, 66], F32)
    FCBC, _ = sb([128, NF], F32)
    SMALL, _ = sb([128, 64], F32)
    ARENA = (off[0] + 31) // 32 * 32
    assert ARENA < SB_END

    class Arena:
        def __init__(self, base):
            self.o = base

        def a(self, shape, dt):
            h, e = sb(shape, dt, at=self.o)
            self.o = (e + 31) // 32 * 32
            assert self.o <= SB_END, ("arena overflow", self.o - SB_END)
            return h

    an = Arena(ARENA)
    JUNK = an.a([128, D], BF16)
    XN = [an.a([128, D], BF16) for _ in range(2)]
    TMPN = [an.a([128, D], F32) for _ in range(2)]
    ATN = [an.a([128, D], F32) for _ in range(2)]
    BTN = [an.a([128, D], F32) for _ in range(2)]
    GNB = an.a([128, D], F32)
    NSET1 = dict(JUNK=JUNK, XN=XN, TMPN=TMPN, ATN=ATN, BTN=BTN, GNB=GNB)
    ap_ = Arena(ARENA)
    GR = [ap_.a([128, 512], F32) for _ in range(2)]
    ABR = [ap_.a([128, 512], F32) for _ in range(2)]
    aa = Arena(ARENA)
    QG = aa.a([128, 18, 128], BF16)
    KG = aa.a([128, 34, 128], BF16)
    VTG = aa.a([128, 34, 128], BF16)
    VAUG = aa.a([128, 34, 2, 65], BF16)
    PT = [aa.a([128, 768], BF16) for _ in range(2)]
    RC = aa.a([128, 8], F32)
    asg = Arena(ARENA)
    CVH = asg.a([128, 2, 2368], BF16)
    DIAG31 = asg.a([128, 31, 128], BF16)
    CVO = [asg.a([128, 512], F32) for _ in range(2)]
    SQ = asg.a([128, 512], F32)
    MS = asg.a([128, 512], F32)
    T1 = asg.a([128, 512], F32)
    LNG = asg.a([128, 256], F32)
    LNB = asg.a([128, 256], F32)
    SGT = [asg.a([128, 512], BF16) for _ in range(2)]
    UT = [asg.a([128, 256], BF16) for _ in range(4)]
    VT = [asg.a([128, 256], F32) for _ in range(4)]
    VB = [asg.a([128, 256], BF16) for _ in range(4)]
    BST = asg.a([128, 16], F32)
    BSV = asg.a([128, 4, 2], F32)
    RSG = asg.a([128, 4], F32)
    aw = Arena(ARENA)
    GT1 = [aw.a([128, D], F32) for _ in range(2)]
    TMPW = [aw.a([128, D], F32) for _ in range(2)]
    ATN2 = [aw.a([128, D], F32) for _ in range(2)]
    BTN2 = [aw.a([128, D], F32) for _ in range(2)]
    GNB2 = aw.a([128, D], F32)
    aw2 = Arena(ATT_BASE)
    JUNK2 = aw2.a([128, D], BF16)
    XN2 = [aw2.a([128, D], BF16) for _ in range(2)]
    TMPN2 = [aw2.a([128, D], F32) for _ in range(2)]
    assert aw2.o <= ATT_BASE + NT * 768 * 2
    NSET2 = dict(JUNK=JUNK2, XN=XN2, TMPN=TMPN2, ATN=ATN2, BTN=BTN2, GNB=GNB2)
    af = Arena(ARENA)
    GT2 = [af.a([128, D], F32) for _ in range(2)]
    GBUF = [af.a([128, 2308], BF16) for _ in range(2)]
    TC = [af.a([128, 512], F32) for _ in range(3)]
    WDN = af.a([128, FG, D], BF16)
    TMPF = af.a([128, D], F32)
    ACTG, _ = sb([128, FG, TOK], BF16, at=ATT_BASE)
    afn = Arena(ARENA)
    FNGT = afn.a([128, D], F32)
    FJUNK = afn.a([128, D], BF16)
    OUTT = [afn.a([128, D], F32) for _ in range(2)]

    PSB = nc.alloc_psum_tensor("psb", [128, 3072], F32)
    PBB = nc.alloc_psum_tensor("pbb", [128, 2048], BF16)

    with ExitStack() as es:
        def sems(n, nm):
            return [es.enter_context(nc.semaphore("%s%d" % (nm, i))) for i in range(n)]

        PE = Eng("pe", nc.tensor, sems(12, "pe"))
        ACT = Eng("act", nc.scalar, sems(12, "ac"))
        DVE = Eng("dve", nc.vector, sems(16, "dv"))
        POOL = Eng("pool", nc.gpsimd, sems(2, "po"))
        SP = Eng("sp", nc.sync, sems(1, "sp"))
        POOL.dsems = sems(16, "pd")
        SP.dsems = sems(16, "sd")
        ALLE = [PE, ACT, DVE, POOL, SP]

        def bar():
            barrier(ALLE)

        Xt = [TT() for _ in range(NT)]
        HTt = [TT() for _ in range(NT)]
        ATTt = [TT() for _ in range(NT)]
        RGt = [TT() for _ in range(3)]
        PSt = [TT() for _ in range(6)]
        PBt = [TT() for _ in range(2)]
        ring_i = [0]
        ps1_i = [0]
        ps2_i = [0]
        pb_i = [0]

        def ring():
            i = ring_i[0] % 3
            ring_i[0] += 1
            return RINGS[i], RGt[i]

        def ps1():
            i = ps1_i[0] % 6
            ps1_i[0] += 1
            return PSB[:, i * 512:(i + 1) * 512], [PSt[i]]

        def ps2():
            i = ps2_i[0] % 3
            ps2_i[0] += 1
            return PSB[:, i * 1024:(i + 1) * 1024], [PSt[2 * i], PSt[2 * i + 1]]

        def pb():
            i = pb_i[0] % 2
            pb_i[0] += 1
            return PBB[:, i * 1024:(i + 1) * 1024], [PBt[i]]

        pre = {}

        def prefetch(key, src, ncols):
            slot, st = ring()
            dma(POOL, slot[:, 0:ncols], src, W=[st])
            pre[key] = (slot, st)

        def getw(key, src, ncols):
            if key in pre:
                return pre.pop(key)
            slot, st = ring()
            dma(POOL, slot[:, 0:ncols], src, W=[st])
            return slot, st

        def ht_tiles(t0, n):
            return HTt[t0 // 128:(t0 + n + 127) // 128]

        cst = TT()
        op(DVE, lambda: nc.vector.memset(ident[:], 1.0), W=[cst])
        op(POOL, lambda: nc.gpsimd.affine_select(out=ident[:], in_=ident[:], pattern=[[-1, 128]],
                                                  compare_op=ALU.is_equal, fill=0.0, base=0,
                                                  channel_multiplier=1), R=[cst], W=[cst])
        op(DVE, lambda: nc.vector.tensor_copy(identf[:], ident[:]), R=[cst], W=[cst])
        op(DVE, lambda: nc.vector.memset(blk64[:], 0.0), W=[cst])
        op(DVE, lambda: nc.vector.memset(blk64[0:64, 0:64], 1.0 / 64), W=[cst])
        op(DVE, lambda: nc.vector.memset(blk64[64:128, 64:128], 1.0 / 64), W=[cst])
        op(DVE, lambda: nc.vector.memset(ones_bf[:], 1.0), W=[cst])
        op(DVE, lambda: nc.vector.memset(epsc[:], EPS), W=[cst])
        sct = TT()
        dma(SP, SCf[:], cT, W=[sct])
        op(ACT, lambda: nc.scalar.activation(out=SC[:], in_=SCf[:], func=AF.Silu), R=[sct], W=[sct])

        def load_x(s):
            for t in range(NT):
                src = x2[s, t * 128:(t + 1) * 128, :] if t < 16 else ctx2[s, (t - 16) * 128:(t - 15) * 128, :]
                dma(SP, X[:, t, :], src, W=[Xt[t]])

        load_x(0)
        modt_l = [TT() for _ in range(nl)]

        def ada_piece(l, piece, slot, st, GRa, ABRa, gtile):
            w = piece // 2
            half = piece % 2
            sv = slot[:, 0:4096].rearrange("p (k n) -> p k n", n=512)
            pr, prt = ps1()
            for k in range(8):
                op(PE, lambda k=k: nc.tensor.matmul(pr[0:3, :], SC[:, k, :], sv[:, k, :],
                                                    start=(k == 0), stop=(k == 7)),
                   R=[st, sct], W=prt, inc=(k == 7))
            dma(SP, ABRa, adab[l, piece * 512:(piece + 1) * 512].partition_broadcast(3), W=[gtile])
            op(DVE, lambda: nc.vector.tensor_tensor(out=GRa, in0=pr[0:3, :], in1=ABRa, op=ALU.add),
               R=prt + [gtile], W=[gtile])
            dma(SP, grow[l, w, :, half * 512:(half + 1) * 512], GRa, R=[gtile], W=[modt_l[l]])

        for piece in range(12):
            slot, st = ring()
            dma(POOL, slot[:, 0:4096], adaw[0, piece], W=[st])
            gi = piece % 2
            ada_piece(0, piece, slot, st, GR[gi][0:3, :], ABR[gi][0:3, :], TT())
        bar()

        def load_params(l, vb):
            pt = TT()
            dma(SP, SGBC[:], sgbc[l], W=[pt])
            dma(SP, CVWC[:], cvwc[l], W=[pt])
            dma(SP, CVPC[:], cvpc[l], W=[pt])
            dma(SP, FCWC[:], fcwc[l], W=[pt])
            dma(SP, FCBC[:], fcbc[l], W=[pt])
            dma(POOL, SGW[:], sgwT[l], W=[pt])
            return pt

        def norm_setup(ntile, ni, l, s, NS):
            stt = dict(NS=NS, jt=TT(), sst=TT(), att=[TT(), TT()], xnt=[TT(), TT()], tnt=[TT(), TT()])
            gnt = TT()
            dma(SP, NS["GNB"][:], (n1g if ni == 0 else n2g)[l].partition_broadcast(128), W=[gnt])
            for vi, v in enumerate((s, 2)):
                if vi == 1 and ntile == 16:
                    continue
                dma(SP, NS["ATN"][vi][:], grow[l, 3 * ni + 1, v, :].partition_broadcast(128), R=[modt_l[l]], W=[stt["att"][vi]])
                dma(SP, NS["BTN"][vi][:], grow[l, 3 * ni, v, :].partition_broadcast(128), R=[modt_l[l]], W=[stt["att"][vi]])
                op(DVE, lambda vi=vi: nc.vector.scalar_tensor_tensor(out=NS["ATN"][vi][:], in0=NS["ATN"][vi][:], scalar=1.0,
                                                                     in1=NS["GNB"][:], op0=ALU.add, op1=ALU.mult),
                   R=[stt["att"][vi], gnt], W=[stt["att"][vi]])
            op(DVE, lambda: nc.vector.memset(SS[:], 0.0), W=[stt["sst"]])
            return stt

        def norm_tile_a(t, stt):
            NS = stt["NS"]
            sst = TT()
            stt.setdefault("sst_t", {})[t] = sst
            op(ACT, lambda: nc.scalar.activation(out=NS["JUNK"][:], in_=X[:, t, :], func=AF.Square,
                                                 accum_out=SS[:, 0, t:t + 1]), R=[Xt[t], stt["sst"]], W=[stt["jt"], sst])
            op(ACT, lambda: nc.scalar.activation(out=SS[:, 1, t:t + 1], in_=SS[:, 0, t:t + 1], func=AF.Sqrt,
                                                 bias=epsc[:], scale=1.0 / D), R=[sst, cst], W=[sst])

        def norm_tile_b1(t, stt):
            NS = stt["NS"]
            sst = stt["sst_t"][t]
            vi = 0 if t < 16 else 1
            xi = t % 2
            op(DVE, lambda: nc.vector.reciprocal(SS[:, 1, t:t + 1], SS[:, 1, t:t + 1]), R=[sst], W=[sst])
            op(DVE, lambda: nc.vector.scalar_tensor_tensor(
                out=NS["TMPN"][xi][:], in0=X[:, t, :], scalar=SS[:, 1, t:t + 1], in1=NS["ATN"][vi][:], op0=ALU.mult, op1=ALU.mult),
               R=[Xt[t], sst, stt["att"][vi]], W=[stt["tnt"][xi]])
            op(DVE, lambda: nc.vector.tensor_tensor(out=NS["XN"][xi][:], in0=NS["TMPN"][xi][:], in1=NS["BTN"][vi][:], op=ALU.add),
               R=[stt["tnt"][xi], stt["att"][vi]], W=[stt["xnt"][xi]])

        def norm_tile_b2(t, stt):
            NS = stt["NS"]
            xi = t % 2
            p, ptk = pb()
            for c in range(8):
                op(PE, lambda c=c: nc.tensor.transpose(p[:, c * 128:(c + 1) * 128],
                                                       NS["XN"][xi][:, c * 128:(c + 1) * 128], ident[:]),
                   R=[stt["xnt"][xi], cst], W=ptk, inc=(c == 7))
            dst = HT[:, :, t * 128:(t + 1) * 128]
            srcp = p[:, 0:1024].rearrange("p (c n) -> p c n", c=8)
            op(ACT, lambda: nc.scalar.copy(dst, srcp), R=ptk, W=[HTt[t]])

        def proj_fm(wv, c0, groups, evac):
            for (t0, n) in groups:
                p, ptk = ps1()
                for k in range(8):
                    op(PE, lambda k=k: nc.tensor.matmul(p[:, 0:n], wv[0][:, k, c0:c0 + 128], HT[:, k, t0:t0 + n],
                                                        start=(k == 0), stop=(k == 7)),
                       R=[wv[1]] + ht_tiles(t0, n), W=ptk, inc=(k == 7))
                evac(p, ptk, t0, n)

        for s in range(nseq):
            if s > 0:
                load_x(s)
            for l in range(nl):
                last = (l == nl - 1)
                ntile = 16 if last else NT
                groups_q = GROUPS[:4] if last else GROUPS
                pt = load_params(l, s)
                nst = norm_setup(NT, 0, l, s, NSET1)
                for t in range(NT + 2):
                    if t < NT:
                        norm_tile_a(t, nst)
                    if 0 <= t - 1 < NT:
                        norm_tile_b1(t - 1, nst)
                    if 0 <= t - 2 < NT:
                        norm_tile_b2(t - 2, nst)
                prefetch(("winA", 0), winA[l, 0], 3072)
                bar()
                qt_t = [TT() for _ in range(5)]
                kt_t = [TT() for _ in range(5)]
                vt_t = [TT() for _ in range(5)]
                va_t = [TT() for _ in range(5)]
                KG4 = KG[:, 0:32, :].rearrange("p (g w) (r c) -> p g w r c", w=4, c=32)
                VG4 = VTG[:, 0:32, :].rearrange("p (g w) (r c) -> p g w r c", w=4, c=32)
                for c in range(4):
                    slot, st = getw(("winA", c), winA[l, c], 3072)
                    wv = (slot[:, 0:3072].rearrange("p (k n) -> p k n", n=384), st)

                    def ev_q(p, ptk, t0, n):
                        gr = t0 // 512
                        if gr < 4:
                            dst = QG[:, 4 * gr:4 * gr + 4, :].rearrange("p b (r c) -> p b r c", c=16)
                            srcp = p[:, 0:512].rearrange("p (r b c) -> p b r c", b=4, c=16)
                        else:
                            dst = QG[:, 16:18, :]
                            srcp = p[:, 0:256].rearrange("p (b n) -> p b n", b=2)
                        op(ACT, lambda: nc.scalar.activation(out=dst, in_=srcp, func=AF.Copy, scale=0.125),
                           R=ptk, W=[qt_t[gr]])

                    def ev_gather(G4, GF, tl, eng0):
                        def ev(p, ptk, t0, n):
                            gr = t0 // 512
                            if gr < 4:
                                for w in range(4):
                                    dst = G4[:, 2 * gr:2 * gr + 2, w, :, :]
                                    srcp = p[:, 0:512].rearrange("p (g r c) -> p g r c", g=2, r=4)[
                                        :, :, :, AC_COL[w]:AC_COL[w] + 32]
                                    if (gr + eng0) % 2 == 0:
                                        op(DVE, lambda: nc.vector.tensor_copy(dst, srcp), R=ptk, W=[tl[gr]])
                                    else:
                                        op(ACT, lambda: nc.scalar.copy(dst, srcp), R=ptk, W=[tl[gr]])
                            else:
                                dst = GF[:, 32:34, :]
                                srcp = p[:, 0:256].rearrange("p (b n) -> p b n", b=2)
                                op(DVE, lambda: nc.vector.tensor_copy(dst, srcp), R=ptk, W=[tl[gr]])
                        return ev

                    proj_fm(wv, 0, groups_q, ev_q)
                    proj_fm(wv, 128, GROUPS, ev_gather(KG4, KG, kt_t, 0))
                    proj_fm(wv, 256, GROUPS, ev_gather(VG4, VTG, vt_t, 1))
                    if c == 0:
                        op(DVE, lambda: nc.vector.memset(VAUG[:, :, :, 64:65], 1.0), W=va_t)
                    for pk in range(5):
                        nk = 8 if pk < 4 else 2
                        p, ptk = pb()
                        for j in range(nk):
                            kt = pk * 8 + j
                            op(PE, lambda j=j, kt=kt: nc.tensor.transpose(p[:, j * 128:(j + 1) * 128], VTG[:, kt, :], ident[:]),
                               R=[vt_t[pk], cst], W=ptk, inc=(j == nk - 1))
                        dst = VAUG[:, pk * 8:pk * 8 + nk, :, 0:64]
                        srcp = p[:, 0:nk * 128].rearrange("p (a b c) -> p a b c", a=nk, b=2)
                        if pk % 2 == 0:
                            op(ACT, lambda dst=dst, srcp=srcp: nc.scalar.copy(dst, srcp), R=ptk, W=[va_t[pk]])
                        else:
                            op(DVE, lambda dst=dst, srcp=srcp: nc.vector.tensor_copy(dst, srcp), R=ptk, W=[va_t[pk]])
                    ptl = [TT(), TT()]
                    for hh in range(2):
                        h = 2 * c + hh
                        P0 = 64 * hh
                        bslot, bst = ring()
                        dma(POOL, bslot[:, 0:4608], btab[l, h], W=[bst])
                        BT = bslot[:, 0:4608].rearrange("p (t q) -> p t q", q=128)
                        nblk = 16 if last else 18
                        BTF = bslot[:, 0:4608]

                        def s_stage(qb):
                            di = qb % 2
                            p = PSB[:, di * 1024:(di + 1) * 1024]
                            ptk = [PSt[2 * di], PSt[2 * di + 1]]
                            pi = qb % 2
                            qa = QG[P0:P0 + 64, qb, :]
                            if qb < 16:
                                rb, cb = qb // 4, qb % 4
                                qtt = [qt_t[rb]]
                                g0 = AR_ROW[rb] // 4
                                ti0 = (PAT[rb] * 3 + PAT[cb]) * 4
                                op(PE, lambda: nc.tensor.matmul(p[:, 0:512], ident[:], BTF[:, ti0 * 128:ti0 * 128 + 512],
                                                                start=True, stop=False), R=[bst, cst], W=ptk, inc=False)
                                for j in range(4):
                                    g = g0 + j
                                    ka = KG[P0:P0 + 64, g * 4 + cb, :]
                                    op(PE, lambda j=j, ka=ka: nc.tensor.matmul(
                                        p[:, j * 128:(j + 1) * 128], ka, qa, start=False, stop=(j == 3),
                                        skip_group_check=True),
                                       R=[kt_t[g // 2]] + qtt, W=ptk, inc=False)
                                lo, nkt = 0, 6
                                kts = [(g0 + j) * 4 + cb for j in range(4)] + [32, 33]
                            else:
                                qtt = [qt_t[4]]
                                lo, nkt = 512, 2
                                kts = [32, 33]
                            for j in range(2):
                                op(PE, lambda j=j: nc.tensor.matmul(
                                    p[:, 512 + j * 128:512 + (j + 1) * 128],
                                    KG[P0:P0 + 64, 32 + j, :], qa, start=True, stop=True),
                                   R=[kt_t[4]] + qtt, W=ptk, inc=(j == 1))
                            op(ACT, lambda: nc.scalar.activation(
                                out=PT[pi][:, lo:lo + nkt * 128], in_=p[:, lo:lo + nkt * 128], func=AF.Exp),
                               R=ptk, W=[ptl[pi]])
                            return (qb, pi, lo, kts)

                        def pv_stage(st_):
                            qb, pi, lo, kts = st_
                            bi = 4 + qb % 2
                            po = PSB[:, bi * 512:(bi + 1) * 512]
                            pot = [PSt[bi]]
                            for ji, kt in enumerate(kts):
                                op(PE, lambda ji=ji, kt=kt: nc.tensor.matmul(
                                    po[:, 0:65], PT[pi][:, lo + ji * 128:lo + (ji + 1) * 128], VAUG[:, kt, hh, :],
                                    start=(ji == 0), stop=(ji == len(kts) - 1)),
                                   R=[ptl[pi], va_t[kt // 8]], W=pot, inc=(ji == len(kts) - 1))
                            rct = TT()
                            ri = qb % 8
                            op(DVE, lambda: nc.vector.reciprocal(RC[:, ri:ri + 1], po[:, 64:65]), R=pot, W=[rct])
                            op(DVE, lambda: nc.vector.tensor_scalar(
                                ATT[:, qb, h * 64:(h + 1) * 64], po[:, 0:64], RC[:, ri:ri + 1], None, ALU.mult),
                               R=pot + [rct], W=[ATTt[qb]])

                        prev = None
                        for qb in range(nblk):
                            cur = s_stage(qb)
                            if prev is not None:
                                pv_stage(prev)
                            prev = cur
                        pv_stage(prev)
                prefetch(("winB", 1), winB[l, 1], 4096)
                prefetch(("winB", 0), winB[l, 0], 4096)
                bar()
                cvt = TT()
                op(DVE, lambda: nc.vector.memset(CVH[:], 0.0), W=[cvt])
                slot, st = getw(("winB", 1), winB[l, 1], 4096)
                wv = (slot[:, 0:4096].rearrange("p (k n) -> p k n", n=512), st)
                groups_c = GROUPS[:4] if last else GROUPS
                sgtt = [TT(), TT()]
                for j in range(2):
                    for gi_, (t0, n) in enumerate(groups_c):
                        pa, pat = ps1()
                        pg, pgt = ps1()
                        for k in range(8):
                            op(PE, lambda k=k: nc.tensor.matmul(pa[:, 0:n], wv[0][:, k, j * 128:(j + 1) * 128],
                                                                HT[:, k, t0:t0 + n], start=(k == 0), stop=(k == 7)),
                               R=[st] + ht_tiles(t0, n), W=pat, inc=(k == 7))
                        for k in range(8):
                            op(PE, lambda k=k: nc.tensor.matmul(pg[:, 0:n], wv[0][:, k, 256 + j * 128:256 + (j + 1) * 128],
                                                                HT[:, k, t0:t0 + n], start=(k == 0), stop=(k == 7)),
                               R=[st] + ht_tiles(t0, n), W=pgt, inc=(k == 7))
                        si = gi_ % 2
                        op(ACT, lambda si=si: nc.scalar.activation(out=SGT[si][:, 0:n], in_=pg[:, 0:n], func=AF.Sigmoid),
                           R=pgt, W=[sgtt[si]])
                        base = 15 + t0 if t0 < SEQ else 2078 + 15 + (t0 - SEQ)
                        op(DVE, lambda si=si, base=base: nc.vector.tensor_tensor(
                            out=CVH[:, j, base:base + n], in0=pa[:, 0:n], in1=SGT[si][:, 0:n], op=ALU.mult),
                           R=pat + [sgtt[si]], W=[cvt])
                slot, st = getw(("winB", 0), winB[l, 0], 4096)
                wv = (slot[:, 0:4096].rearrange("p (k n) -> p k n", n=512), st)
                lnt = TT()
                dma(SP, LNG[:], sglng[l].partition_broadcast(128), W=[lnt])
                dma(SP, LNB[:], sglnb[l].partition_broadcast(128), W=[lnt])
                utt = [TT() for _ in range(4)]
                vtt = [TT() for _ in range(4)]
                vbt = [TT() for _ in range(4)]
                bsvt = [TT(), TT()]
                rsgt = [TT(), TT()]
                bstt = TT()

                def sg_a(t):
                    i4 = t % 4
                    p, ptk = ps1()
                    for k in range(8):
                        op(PE, lambda k=k: nc.tensor.matmul(p[:, 0:512], HT[:, k, t * 128:(t + 1) * 128], wv[0][:, k, :],
                                                            start=(k == 0), stop=(k == 7)),
                           R=[st, HTt[t]], W=ptk, inc=(k == 7))
                    op(ACT, lambda: nc.scalar.activation(out=UT[i4][:], in_=p[:, 0:256], func=AF.Gelu_apprx_tanh),
                       R=ptk, W=[utt[i4]])
                    op(ACT, lambda: nc.scalar.activation(out=VT[i4][:], in_=p[:, 256:512], func=AF.Gelu_apprx_tanh),
                       R=ptk, W=[vtt[i4]])
                    op(DVE, lambda: nc.vector.bn_stats(BST[:, 0:6], VT[i4][:]), R=[vtt[i4]], W=[bstt])
                    op(DVE, lambda: nc.vector.bn_aggr(BSV[:, i4, :], BST[:, 0:6]), R=[bstt], W=[bsvt[i4 // 2]])

                def sg_b(t0, n):
                    h2 = (t0 % 4) // 2
                    op(ACT, lambda: nc.scalar.activation(out=RSG[:, 2 * h2:2 * h2 + n], in_=BSV[:, 2 * h2:2 * h2 + n, 1], func=AF.Sqrt,
                                                         bias=epsc[:], scale=1.0), R=[bsvt[h2], cst], W=[rsgt[h2]])
                    op(DVE, lambda: nc.vector.reciprocal(RSG[:, 2 * h2:2 * h2 + n], RSG[:, 2 * h2:2 * h2 + n]), R=[rsgt[h2]], W=[rsgt[h2]])
                    for t in range(t0, t0 + n):
                        i4 = t % 4
                        op(DVE, lambda: nc.vector.tensor_scalar(VT[i4][:], VT[i4][:], BSV[:, i4, 0:1], RSG[:, i4:i4 + 1],
                                                                ALU.subtract, ALU.mult), R=[vtt[i4], bsvt[h2], rsgt[h2]], W=[vtt[i4]])
                        op(DVE, lambda: nc.vector.tensor_tensor(out=VT[i4][:], in0=VT[i4][:], in1=LNG[:], op=ALU.mult),
                           R=[vtt[i4], lnt], W=[vtt[i4]])
                        op(DVE, lambda: nc.vector.tensor_tensor(out=VB[i4][:], in0=VT[i4][:], in1=LNB[:], op=ALU.add),
                           R=[vtt[i4], lnt], W=[vbt[i4]])

                def sg_mix(t):
                    i2 = t % 4
                    pm, pmt = ps1()
                    for g in range(4):
                        op(PE, lambda g=g: nc.tensor.matmul(pm[:, g * 64:(g + 1) * 64], SGW[:, g * 128:(g + 1) * 128],
                                                            VB[i2][:, g * 64:(g + 1) * 64], start=True, stop=True),
                           R=[vbt[i2], pt], W=pmt, inc=(g == 3))
                    for g in range(4):
                        op(DVE, lambda g=g: nc.vector.scalar_tensor_tensor(
                            out=ATT[:, t, 512 + g * 64:512 + (g + 1) * 64], in0=pm[:, g * 64:(g + 1) * 64],
                            scalar=SGBC[:, g:g + 1], in1=UT[i2][:, g * 64:(g + 1) * 64], op0=ALU.add, op1=ALU.mult),
                           R=pmt + [utt[i2], pt], W=[ATTt[t]])

                dgt = TT()
                cvo_t = [TT(), TT()]
                sqt, mst, t1t = TT(), TT(), TT()
                cvi = [0]

                def cv_diag(j):
                    for k in range(31):
                        op(DVE, lambda k=k: nc.vector.tensor_scalar(DIAG31[:, k, :], ident[:], CVWC[:, j * 31 + k:j * 31 + k + 1],
                                                                    None, ALU.mult), R=[cst, pt], W=[dgt])

                def cv_conv(j, gi_):
                    t0, n = groups_c[gi_]
                    base = t0 if t0 < SEQ else 2078 + (t0 - SEQ)
                    pc, pct = ps1()
                    for k in range(31):
                        op(PE, lambda k=k: nc.tensor.matmul(pc[:, 0:n], DIAG31[:, k, :], CVH[:, j, base + k:base + k + n],
                                                            start=(k == 0), stop=(k == 30)),
                           R=[dgt, cvt], W=pct, inc=(k == 30))
                    ci = cvi[0] % 2
                    cvi[0] += 1
                    op(ACT, lambda: nc.scalar.activation(out=CVO[ci][:, 0:n], in_=pc[:, 0:n], func=AF.Identity,
                                                         bias=CVPC[:, j:j + 1], scale=1.0), R=pct + [pt], W=[cvo_t[ci]])
                    return (j, t0, n, ci)

                def cv_tail(h_):
                    j, t0, n, ci = h_
                    op(ACT, lambda: nc.scalar.activation(out=SQ[:, 0:n], in_=CVO[ci][:, 0:n], func=AF.Square),
                       R=[cvo_t[ci]], W=[sqt])
                    pme, pmet = ps1()
                    pq, pqt = ps1()
                    op(PE, lambda: nc.tensor.matmul(pme[:, 0:n], blk64[:], CVO[ci][:, 0:n], start=True, stop=True),
                       R=[cvo_t[ci], cst], W=pmet, inc=True)
                    op(PE, lambda: nc.tensor.matmul(pq[:, 0:n], blk64[:], SQ[:, 0:n], start=True, stop=True),
                       R=[sqt, cst], W=pqt, inc=True)
                    op(ACT, lambda: nc.scalar.copy(MS[:, 0:n], pme[:, 0:n]), R=pmet, W=[mst])
                    op(DVE, lambda: nc.vector.tensor_tensor(out=T1[:, 0:n], in0=MS[:, 0:n], in1=MS[:, 0:n], op=ALU.mult),
                       R=[mst], W=[t1t])
                    op(DVE, lambda: nc.vector.tensor_tensor(out=T1[:, 0:n], in0=pq[:, 0:n], in1=T1[:, 0:n], op=ALU.subtract),
                       R=pqt + [t1t], W=[t1t])
                    op(ACT, lambda: nc.scalar.activation(out=T1[:, 0:n], in_=T1[:, 0:n], func=AF.Sqrt, bias=epsc[:], scale=1.0),
                       R=[t1t, cst], W=[t1t])
                    op(DVE, lambda: nc.vector.reciprocal(T1[:, 0:n], T1[:, 0:n]), R=[t1t], W=[t1t])
                    op(DVE, lambda: nc.vector.tensor_tensor(out=CVO[ci][:, 0:n], in0=CVO[ci][:, 0:n], in1=MS[:, 0:n],
                                                            op=ALU.subtract), R=[cvo_t[ci], mst], W=[cvo_t[ci]])
                    op(DVE, lambda: nc.vector.tensor_tensor(out=CVO[ci][:, 0:n], in0=CVO[ci][:, 0:n], in1=T1[:, 0:n],
                                                            op=ALU.mult), R=[cvo_t[ci], t1t], W=[cvo_t[ci]])
                    op(ACT, lambda: nc.scalar.activation(out=HT[:, 6 + j, t0:t0 + n], in_=CVO[ci][:, 0:n], func=AF.Silu,
                                                         scale=CVPC[:, 2 + j:3 + j], bias=CVPC[:, 4 + j:5 + j]),
                       R=[cvo_t[ci], pt], W=ht_tiles(t0, n))

                cv_diag(0)
                pend_mix = []
                pend_cv = None
                pairs = [(t, min(2, ntile - t)) for t in range(0, ntile, 2)]
                for pi_, (t0, n) in enumerate(pairs):
                    for t in range(t0, t0 + n):
                        sg_a(t)
                    for t in pend_mix:
                        sg_mix(t)
                    sg_b(t0, n)
                    pend_mix = list(range(t0, t0 + n))
                    if pi_ % 2 == 1 and pi_ // 2 < len(groups_c):
                        cur = cv_conv(0, pi_ // 2)
                        if pend_cv is not None:
                            cv_tail(pend_cv)
                        pend_cv = cur
                for t in pend_mix:
                    sg_mix(t)
                if len(pairs) // 2 < len(groups_c):
                    cur = cv_conv(0, len(groups_c) - 1)
                    if pend_cv is not None:
                        cv_tail(pend_cv)
                    pend_cv = cur
                def tr_att(qb):
                    p, ptk = pb()
                    for c in range(4):
                        op(PE, lambda c=c: nc.tensor.transpose(p[:, c * 128:(c + 1) * 128], ATT[:, qb, c * 128:(c + 1) * 128],
                                                               ident[:]), R=[ATTt[qb], cst], W=ptk, inc=(c == 3))
                    if qb < 16:
                        rb, cb = qb // 4, qb % 4
                        dst = HT[:, 0:4, 0:SEQ].rearrange("p c (r w) -> p c r w", w=64)[:, :, 8 * rb:8 * rb + 8, 16 * cb:16 * cb + 16]
                        srcp = p[:, 0:512].rearrange("p (c r w) -> p c r w", c=4, r=8)
                        wt = HTt[4 * rb:4 * rb + 4]
                    else:
                        dst = HT[:, 0:4, SEQ + (qb - 16) * 128:SEQ + (qb - 15) * 128]
                        srcp = p[:, 0:512].rearrange("p (c n) -> p c n", c=4)
                        wt = [HTt[qb]]
                    if qb % 2 == 0:
                        op(ACT, lambda: nc.scalar.copy(dst, srcp), R=ptk, W=wt)
                    else:
                        op(DVE, lambda: nc.vector.tensor_copy(dst, srcp), R=ptk, W=wt)

                def tr_sg(t):
                    p, ptk = pb()
                    for c in range(2):
                        op(PE, lambda c=c: nc.tensor.transpose(p[:, c * 128:(c + 1) * 128],
                                                               ATT[:, t, 512 + c * 128:512 + (c + 1) * 128], ident[:]),
                           R=[ATTt[t], cst], W=ptk, inc=(c == 1))
                    dst = HT[:, 4:6, t * 128:(t + 1) * 128]
                    srcp = p[:, 0:256].rearrange("p (c n) -> p c n", c=2)
                    if t % 2 == 1:
                        op(ACT, lambda: nc.scalar.copy(dst, srcp), R=ptk, W=[HTt[t]])
                    else:
                        op(DVE, lambda: nc.vector.tensor_copy(dst, srcp), R=ptk, W=[HTt[t]])

                trs = [(tr_att, i) for i in range(ntile)] + [(tr_sg, i) for i in range(ntile)]
                per = (len(trs) + len(groups_c) - 1) // len(groups_c)
                cv_diag(1)
                for gi_ in range(len(groups_c)):
                    cur = cv_conv(1, gi_)
                    cv_tail(pend_cv)
                    pend_cv = cur
                    for _ in range(per):
                        if trs:
                            f_, i_ = trs.pop(0)
                            f_(i_)
                cv_tail(pend_cv)
                while trs:
                    f_, i_ = trs.pop(0)
                    f_(i_)
                prefetch(("wout", 0), woutd[l, 0], 4096)
                prefetch(("wout", 1), woutd[l, 1], 4096)
                bar()
                gtt = TT()
                dma(SP, GT1[0][:], grow[l, 2, s, :].partition_broadcast(128), R=[modt_l[l]], W=[gtt])
                dma(SP, GT1[1][:], grow[l, 2, 2, :].partition_broadcast(128), R=[modt_l[l]], W=[gtt])
                nst = norm_setup(ntile, 1, l, s, NSET2)
                wos = []
                for half in range(2):
                    slot, st = getw(("wout", half), woutd[l, half], 4096)
                    wos.append((slot[:, 0:4096].rearrange("p (k n) -> p k n", n=512), st))
                tmt = [TT(), TT()]
                for t in range(ntile + 3):
                    if t < ntile:
                        p, ptk = ps2()
                        for half in range(2):
                            for k in range(8):
                                op(PE, lambda k=k, half=half: nc.tensor.matmul(
                                    p[:, half * 512:(half + 1) * 512], HT[:, k, t * 128:(t + 1) * 128], wos[half][0][:, k, :],
                                    start=(k == 0), stop=(k == 7)),
                                   R=[wos[half][1], HTt[t]], W=ptk, inc=(k == 7 and half == 1))
                    if 0 <= t - 3 < ntile:
                        norm_tile_b2(t - 3, nst)
                    if t < ntile:
                        vi = 0 if t < 16 else 1
                        i2 = t % 2
                        op(DVE, lambda: nc.vector.tensor_tensor(out=TMPW[i2][:], in0=p[:, :], in1=GT1[vi][:], op=ALU.mult),
                           R=ptk + [gtt], W=[tmt[i2]])
                        op(DVE, lambda: nc.vector.tensor_tensor(out=X[:, t, :], in0=X[:, t, :], in1=TMPW[i2][:], op=ALU.add),
                           R=[tmt[i2], Xt[t]], W=[Xt[t]])
                        norm_tile_a(t, nst)
                    if 0 <= t - 1 < ntile:
                        norm_tile_b1(t - 1, nst)
                prefetch(("ffn", 0), wffn[l, 0, :, 0:2048], 2048)
                bar()
                gtt = TT()
                dma(SP, GT2[0][:], grow[l, 5, s, :].partition_broadcast(128), R=[modt_l[l]], W=[gtt])
                dma(SP, GT2[1][:], grow[l, 5, 2, :].partition_broadcast(128), R=[modt_l[l]], W=[gtt])
                gbt = [TT(), TT()]
                for i in range(2):
                    op(DVE, lambda i=i: nc.vector.memset(GBUF[i][:], 0.0), W=[gbt[i]])
                tct = [TT(), TT(), TT()]
                tcn = [0]
                wdt = [TT() for _ in range(FG)]
                agt = [TT() for _ in range(FG)]
                tft = TT()
                groups_f = GROUPS[:4] if last else GROUPS
                fgroups = [list(range(0, 6)), list(range(6, 12)), list(range(12, 17)), list(range(17, 22))]
                for fg in fgroups:
                    for fi, f in enumerate(fg):
                        slot, st = getw(("ffn", f), wffn[l, f, :, 0:2048], 2048)
                        if f + 1 < NF:
                            prefetch(("ffn", f + 1), wffn[l, f + 1, :, 0:2048], 2048)
                        dma(POOL, WDN[:, fi, :], wffn[l, f, :, 2048:3072], W=[wdt[fi]])
                        wv = (slot[:, 0:2048].rearrange("p (k n) -> p k n", n=256), st)
                        gi2 = f % 2
                        GB = GBUF[gi2]

                        def ev_g(p, ptk, t0, n):
                            base = 1 + t0 if t0 < SEQ else 2051 + (t0 - SEQ)
                            op(ACT, lambda: nc.scalar.copy(GB[:, base:base + n], p[:, 0:n]), R=ptk, W=[gbt[gi2]])

                        proj_fm(wv, 0, groups_f, ev_g)
                        if s == 0 and l + 1 < nl:
                            if 1 <= f <= 12:
                                aslot, ast = getw(("ada", f - 1), adaw[l + 1, f - 1], 4096)
                                ada_piece(l + 1, f - 1, aslot, ast, TMPF[0:3, 0:512], TMPF[0:3, 512:1024], tft)
                        w0c = FCWC[:, f * 3:f * 3 + 1]
                        w1c = FCWC[:, f * 3 + 1:f * 3 + 2]
                        w2c = FCWC[:, f * 3 + 2:f * 3 + 3]

                        def fin(pd):
                            ti, pu, put, t0, n = pd
                            op(ACT, lambda: nc.scalar.activation(out=TC[ti][:, 0:n], in_=TC[ti][:, 0:n], func=AF.Silu),
                               R=[tct[ti]], W=[tct[ti]])
                            op(DVE, lambda: nc.vector.tensor_tensor(out=ACTG[:, fi, t0:t0 + n], in0=pu[:, 0:n],
                                                                    in1=TC[ti][:, 0:n], op=ALU.mult),
                               R=put + [tct[ti]], W=[agt[fi]])

                        pend = None
                        for gi_, (t0, n) in enumerate(groups_f):
                            base = t0 if t0 < SEQ else 2050 + (t0 - SEQ)
                            ti = tcn[0] % 3
                            tcn[0] += 1
                            op(ACT, lambda: nc.scalar.activation(out=TC[ti][:, 0:n], in_=GB[:, base + 1:base + 1 + n], func=AF.Identity,
                                                                 scale=w1c, bias=FCBC[:, f:f + 1]), R=[gbt[gi2], pt], W=[tct[ti]])
                            op(DVE, lambda: nc.vector.scalar_tensor_tensor(out=TC[ti][:, 0:n], in0=GB[:, base:base + n], scalar=w0c,
                                                                            in1=TC[ti][:, 0:n], op0=ALU.mult, op1=ALU.add),
                               R=[gbt[gi2], pt, tct[ti]], W=[tct[ti]])
                            op(DVE, lambda: nc.vector.scalar_tensor_tensor(out=TC[ti][:, 0:n], in0=GB[:, base + 2:base + 2 + n], scalar=w2c,
                                                                            in1=TC[ti][:, 0:n], op0=ALU.mult, op1=ALU.add),
                               R=[gbt[gi2], pt, tct[ti]], W=[tct[ti]])
                            pu, put = ps1()
                            for k in range(8):
                                op(PE, lambda k=k: nc.tensor.matmul(pu[:, 0:n], wv[0][:, k, 128:256], HT[:, k, t0:t0 + n],
                                                                    start=(k == 0), stop=(k == 7)),
                                   R=[st] + ht_tiles(t0, n), W=put, inc=(k == 7))
                            if pend is not None:
                                fin(pend)
                            pend = (ti, pu, put, t0, n)
                        fin(pend)
                        if s == 0 and l + 1 < nl and f <= 11:
                            prefetch(("ada", f), adaw[l + 1, f], 4096)
                    ng = len(fg)
                    for t in range(ntile):
                        p, ptk = ps2()
                        for half in range(2):
                            for fi in range(ng):
                                op(PE, lambda fi=fi, half=half: nc.tensor.matmul(
                                    p[:, half * 512:(half + 1) * 512], ACTG[:, fi, t * 128:(t + 1) * 128],
                                    WDN[:, fi, half * 512:(half + 1) * 512], start=(fi == 0), stop=(fi == ng - 1)),
                                   R=[agt[fi], wdt[fi]], W=ptk, inc=(fi == ng - 1 and half == 1))
                        vi = 0 if t < 16 else 1
                        op(DVE, lambda vi=vi: nc.vector.tensor_tensor(out=TMPF[:], in0=p[:, :], in1=GT2[vi][:], op=ALU.mult),
                           R=ptk + [gtt], W=[tft])
                        op(DVE, lambda t=t: nc.vector.tensor_tensor(out=X[:, t, :], in0=X[:, t, :], in1=TMPF[:], op=ALU.add),
                           R=[tft, Xt[t]], W=[Xt[t]])
                bar()
            fgt = TT()
            dma(SP, FNGT[:], fng.partition_broadcast(128), W=[fgt])
            jt, sst = TT(), TT()
            op(DVE, lambda: nc.vector.memset(SS[:], 0.0), W=[sst])
            for t in range(16):
                op(ACT, lambda t=t: nc.scalar.activation(out=FJUNK[:], in_=X[:, t, :], func=AF.Square,
                                                         accum_out=SS[:, 0, t:t + 1]), R=[Xt[t]], W=[jt, sst])
            op(ACT, lambda: nc.scalar.activation(out=SS[:, 1, 0:16], in_=SS[:, 0, 0:16], func=AF.Sqrt, bias=epsc[:],
                                                 scale=1.0 / D), R=[sst, cst], W=[sst])
            op(DVE, lambda: nc.vector.reciprocal(SS[:, 1, 0:16], SS[:, 1, 0:16]), R=[sst], W=[sst])
            ott = [TT(), TT()]
            for t in range(16):
                i2 = t % 2
                op(DVE, lambda t=t, i2=i2: nc.vector.scalar_tensor_tensor(out=OUTT[i2][:], in0=X[:, t, :], scalar=SS[:, 1, t:t + 1],
                                                                          in1=FNGT[:], op0=ALU.mult, op1=ALU.mult),
                   R=[Xt[t], sst, fgt], W=[ott[i2]])
                dma(SP, y2[s, t * 128:(t + 1) * 128, :], OUTT[i2][:], R=[ott[i2]])
            bar()
        k = SP.dcount
        ns = len(SP.dsems)
        for kk in range(max(0, k - ns), k):
            need(SP, ("d", SP.dsems[kk % ns], 16 * (kk // ns + 1)))
    return nc


def _btab(rpb):
    nlay = rpb.shape[0]
    tab = np.empty((nlay, 8, 128, 36, 128), np.float32)
    RB = [0, 1, 3]
    for rp in range(3):
        rb = RB[rp]
        for j in range(4):
            kr = AR_ROW[rb] + 4 * j + np.arange(4)
            qr = 8 * rb + np.arange(8)
            sr = np.clip(qr - 4, 0, 24)
            mr = (kr[:, None] >= sr[None, :]) & (kr[:, None] < sr[None, :] + 8)
            ri = np.clip(kr[:, None] - qr[None, :] + 7, 0, 14)
            for cp in range(3):
                cb = RB[cp]
                kc = AC_COL[cb] + np.arange(32)
                qc = 16 * cb + np.arange(16)
                sc = np.clip(qc - 8, 0, 48)
                mc = (kc[:, None] >= sc[None, :]) & (kc[:, None] < sc[None, :] + 16)
                ci = np.clip(kc[:, None] - qc[None, :] + 15, 0, 30)
                R = np.broadcast_to(ri[:, None, :, None], (4, 32, 8, 16))
                C = np.broadcast_to(ci[None, :, None, :], (4, 32, 8, 16))
                vals = rpb[:, :, R, C]
                m = mr[:, None, :, None] & mc[None, :, None, :]
                vals = np.where(m[None, None], vals, np.float32(-30000.0))
                ti = (rp * 3 + cp) * 4 + j
                tab[:, :, :, ti, :] = vals.reshape(nlay, 8, 128, 128)
    return tab.reshape(nlay, 8, 128, 4608)


def _prep_shared(i, nl=NL):
    f = np.float32
    c_ = np.ascontiguousarray
    sh = {}
    aw = i["ada_w"][:nl].reshape(nl, 8, 128, 12, 512)
    sh["adaw"] = c_(aw.transpose(0, 3, 2, 1, 4)).reshape(nl, 12, 128, 4096)
    sh["adab"] = c_(i["ada_b"][:nl])
    sh["n1g"] = c_(i["norm1_g"][:nl])
    sh["n2g"] = c_(i["norm2_g"][:nl])
    sh["fng"] = c_(i["final_norm_g"])
    wi = i["w_in"][:nl].reshape(nl, 8, 128, 2560)
    pa = []
    for c in range(4):
        cols = np.concatenate([np.arange(128 * c, 128 * c + 128), 512 + np.arange(128 * c, 128 * c + 128),
                               1024 + np.arange(128 * c, 128 * c + 128)])
        pa.append(wi[:, :, :, cols].transpose(0, 2, 1, 3).reshape(nl, 128, 3072))
    sh["winA"] = c_(np.stack(pa, 1))
    pb_ = [wi[:, :, :, 1536:2048].transpose(0, 2, 1, 3).reshape(nl, 128, 4096),
           wi[:, :, :, 2048:2560].transpose(0, 2, 1, 3).reshape(nl, 128, 4096)]
    sh["winB"] = c_(np.stack(pb_, 1))
    sh["btab"] = _btab(i["na_rpb"][:nl])
    sh["sglng"] = c_(i["sg_norm_g"][:nl])
    sh["sglnb"] = c_(i["sg_norm_b"][:nl])
    sh["sgwT"] = c_(i["sg_w"][:nl].transpose(0, 3, 1, 2)).reshape(nl, 128, 512)
    sh["sgbc"] = c_(i["sg_b"][:nl].transpose(0, 2, 1))
    sh["cvwc"] = c_(i["cv_w"][:nl].reshape(nl, 31, 2, 128).transpose(0, 3, 2, 1)).reshape(nl, 128, 62)
    cvp = np.stack([i["cv_b"][:nl], i["cv_norm_g"][:nl], i["cv_norm_b"][:nl]], 1)
    sh["cvpc"] = c_(cvp.reshape(nl, 3, 2, 128).transpose(0, 3, 1, 2)).reshape(nl, 128, 6)
    wo = i["w_out"][:nl].reshape(nl, 8, 128, 2, 512)
    sh["woutd"] = c_(wo.transpose(0, 3, 2, 1, 4)).reshape(nl, 2, 128, 4096)
    up = i["ffn_w_up"][:nl].reshape(nl, 8, 128, 2, NF, 128)
    upf = up.transpose(0, 4, 2, 1, 3, 5).reshape(nl, NF, 128, 2048)
    dn = i["ffn_w_down"][:nl].reshape(nl, NF, 128, 1024)
    sh["wffn"] = c_(np.concatenate([upf, dn], axis=3))
    sh["fcwc"] = c_(i["ffn_conv_w"][:nl].reshape(nl, 3, NF, 128).transpose(0, 3, 2, 1)).reshape(nl, 128, 66)
    sh["fcbc"] = c_(i["ffn_conv_b"][:nl].reshape(nl, NF, 128).transpose(0, 2, 1))
    return {k: np.asarray(v, f) for k, v in sh.items()}


def _run(inputs, nl=NL, nseq=2, ncores=8):
    i = {k: np.asarray(v) for k, v in inputs.items()}
    sh = _prep_shared(i, nl)
    nc = build(nl, nseq)
    in_maps = []
    for core in range(ncores):
        b0 = 2 * core
        m = dict(sh)
        m["x2"] = np.ascontiguousarray(i["x"][b0:b0 + 2])
        m["ctx2"] = np.ascontiguousarray(i["ctx"][b0:b0 + 2])
        c3 = np.stack([i["c"][b0], i["c"][b0 + 1], i["c_ctx"]], 0)
        m["cT"] = np.ascontiguousarray(c3.reshape(3, 8, 128).transpose(2, 1, 0)).astype(np.float32)
        in_maps.append(m)
    res = run_bass_kernel_spmd(nc, in_maps, core_ids=list(range(ncores)))
    out = np.concatenate([r["y2"] for r in res.results], axis=0)
    return out.astype(np.float32)


def kernel(**inputs):
    return _run(inputs)
```

```python
import numpy as np
from bisect import bisect_left
from contextlib import ExitStack
import concourse.bass as bass
import concourse.mybir as mybir
from concourse.bass_utils import run_bass_kernel_spmd

F32 = mybir.dt.float32
BF16 = mybir.dt.bfloat16
AF = mybir.ActivationFunctionType
ALU = mybir.AluOpType

D = 1024
SEQ = 2048
CTXL = 256
NL = 4
TOK = SEQ + CTXL
NT = TOK // 128
NF = 22
FG = 6
EPS = 1e-6
SB_BASE = 16512
SB_END = 229344
AR_ROW = [0, 4, 12, 16]
AC_COL = [0, 8, 24, 32]
PAT = [0, 1, 1, 2]
GROUPS = [(0, 512), (512, 512), (1024, 512), (1536, 512), (2048, 256)]


class TT:
    __slots__ = ("w", "r")

    def __init__(self):
        self.w = None
        self.r = {}


class Eng:
    def __init__(self, name, obj, sems, lim=4000):
        self.name = name
        self.obj = obj
        self.sems = sems
        self.si = 0
        self.cnt = 0
        self.lim = lim
        self.ins = []
        self.incidx = []
        self.inctok = []
        self.waited = {}
        self.dsems = []
        self.dcount = 0

    def add(self, ins, inc):
        self.ins.append(ins)
        i = len(self.ins) - 1
        if inc:
            self._inc(i)
        return i

    def _inc(self, i):
        if self.cnt >= self.lim:
            self.si += 1
            self.cnt = 0
        sem = self.sems[self.si]
        self.cnt += 1
        self.ins[i].then_inc(sem, 1)
        self.incidx.append(i)
        self.inctok.append((sem, self.cnt))

    def resolve(self, i):
        j = bisect_left(self.incidx, i)
        if j == len(self.incidx):
            self._inc(len(self.ins) - 1)
        return self.inctok[j]


def need(E, ref):
    if ref[0] == "e":
        sem, val = ref[1].resolve(ref[2])
    else:
        sem, val = ref[1], ref[2]
    k = sem.num
    if E.waited.get(k, 0) >= val:
        return
    E.obj.wait_ge(sem, val)
    E.waited[k] = val


def _deps(E, R, W):
    for t in R:
        if t.w is not None:
            need(E, t.w)
    for t in W:
        if t.w is not None and not (t.w[0] == "e" and t.w[1] is E):
            need(E, t.w)
        for rn, rt in t.r.items():
            if rn != E.name:
                need(E, rt)


def op(E, fn, R=(), W=(), inc=True):
    _deps(E, R, W)
    ins = fn()
    idx = E.add(ins, inc)
    ref = ("e", E, idx)
    for t in R:
        t.r[E.name] = ref
    for t in W:
        t.w = ref
        t.r = {}
    return ref


_dma_uid = [0]


def dma(Q, out, in_, R=(), W=()):
    k = Q.dcount
    ns = len(Q.dsems)
    sem = Q.dsems[k % ns]
    val = 16 * (k // ns + 1)
    if k >= ns:
        need(Q, ("d", sem, val - 16))
    _deps(Q, R, W)
    Q.obj.dma_start(out=out, in_=in_).then_inc(sem, 16)
    Q.dcount += 1
    ref = ("d", sem, val)
    _dma_uid[0] += 1
    for t in R:
        t.r["dma%d" % _dma_uid[0]] = ref
    for t in W:
        t.w = ref
        t.r = {}
    return ref


def barrier(engs):
    refs = []
    for F in engs:
        if F.ins:
            refs.append(("e", F, len(F.ins) - 1))
        if F.dcount:
            k = F.dcount
            ns = len(F.dsems)
            for kk in range(max(0, k - ns), k):
                refs.append(("d", F.dsems[kk % ns], 16 * (kk // ns + 1)))
    for E in engs:
        for r in refs:
            need(E, r)


def build(nl=NL, nseq=2):
    nc = bass.Bass("TRN2", target_bir_lowering=False)

    def din(name, shape):
        return nc.dram_tensor(name, list(shape), F32, kind="ExternalInput").ap()

    x2 = din("x2", [2, SEQ, D])
    ctx2 = din("ctx2", [2, CTXL, D])
    cT = din("cT", [128, 8, 3])
    adaw = din("adaw", [nl, 12, 128, 4096])
    adab = din("adab", [nl, 6144])
    n1g = din("n1g", [nl, D])
    n2g = din("n2g", [nl, D])
    fng = din("fng", [D])
    winA = din("winA", [nl, 4, 128, 3072])
    winB = din("winB", [nl, 2, 128, 4096])
    btab = din("btab", [nl, 8, 128, 4608])
    sglng = din("sglng", [nl, 256])
    sglnb = din("sglnb", [nl, 256])
    sgwT = din("sgwT", [nl, 128, 512])
    sgbc = din("sgbc", [nl, 128, 4])
    cvwc = din("cvwc", [nl, 128, 62])
    cvpc = din("cvpc", [nl, 128, 6])
    woutd = din("woutd", [nl, 2, 128, 4096])
    wffn = din("wffn", [nl, NF, 128, 3072])
    fcwc = din("fcwc", [nl, 128, 66])
    fcbc = din("fcbc", [nl, 128, NF])
    y2 = nc.dram_tensor("y2", [2, SEQ, D], F32, kind="ExternalOutput").ap()
    grow = nc.dram_tensor("grow", [nl, 6, 3, 1024], F32).ap()

    off = [SB_BASE]
    uid = [0]

    def sb(shape, dt, at=None):
        nb = int(np.prod(shape[1:])) * (4 if dt == F32 else 2)
        o = off[0] if at is None else at
        o = (o + 31) // 32 * 32
        uid[0] += 1
        h = nc.alloc_sbuf_tensor_at("t%d" % uid[0], list(shape), dt, offset=o)
        if at is None:
            off[0] = o + nb
        return h, o + nb

    X, _ = sb([128, NT, D], F32)
    HT, _ = sb([128, 8, TOK], BF16)
    ATT_BASE = off[0]
    ATT, _ = sb([128, NT, 768], BF16)
    RINGS = [sb([128, 4608], BF16)[0] for _ in range(3)]
    ident, _ = sb([128, 128], BF16)
    identf, _ = sb([128, 128], F32)
    blk64, _ = sb([128, 128], F32)
    ones_bf, _ = sb([128, 128], BF16)
    epsc, _ = sb([128, 1], F32)
    SC, _ = sb([128, 8, 3], BF16)
    SCf, _ = sb([128, 8, 3], F32)
    MODC, _ = sb([128, NL, 32, 3], F32)
    ADABC, _ = sb([128, 48], F32)
    G1C, _ = sb([128, 8], F32)
    G2C, _ = sb([128, 8], F32)
    AB, _ = sb([128, 2, 4, 8], F32)
    SS, _ = sb([128, 2, NT], F32)
    SGW, _ = sb([128, 512], BF16)
    SGBC, _ = sb([128, 4], F32)
    CVWC, _ = sb([128, 62], F32)
    CVPC, _ = sb([128, 6], F32)
    FCWC, _ = sb([128, 66], F32)
    FCBC, _ = sb([128, NF], F32)
    SMALL, _ = sb([128, 64], F32)
    ARENA = (off[0] + 31) // 32 * 32
    assert ARENA < SB_END

    class Arena:
        def __init__(self, base):
            self.o = base

        def a(self, shape, dt):
            h, e = sb(shape, dt, at=self.o)
            self.o = (e + 31) // 32 * 32
            assert self.o <= SB_END, ("arena overflow", self.o - SB_END)
            return h

    an = Arena(ARENA)
    JUNK = an.a([128, D], BF16)
    XN = [an.a([128, D], BF16) for _ in range(2)]
    TMPN = [an.a([128, D], F32) for _ in range(2)]
    ATN = [an.a([128, D], F32) for _ in range(2)]
    BTN = [an.a([128, D], F32) for _ in range(2)]
    GNB = an.a([128, D], F32)
    NSET1 = dict(JUNK=JUNK, XN=XN, TMPN=TMPN, ATN=ATN, BTN=BTN, GNB=GNB)
    ap_ = Arena(ARENA)
    GR = [ap_.a([128, 512], F32) for _ in range(2)]
    ABR = [ap_.a([128, 512], F32) for _ in range(2)]
    aa = Arena(ARENA)
    QG = aa.a([128, 18, 128], BF16)
    KG = aa.a([128, 34, 128], BF16)
    VTG = aa.a([128, 34, 128], BF16)
    VAUG = aa.a([128, 34, 2, 65], BF16)
    PT = [aa.a([128, 768], BF16) for _ in range(2)]
    RC = aa.a([128, 8], F32)
    asg = Arena(ARENA)
    CVH = asg.a([128, 2, 2368], BF16)
    DIAG31 = asg.a([128, 31, 128], BF16)
    CVO = [asg.a([128, 512], F32) for _ in range(2)]
    SQ = asg.a([128, 512], F32)
    MS = asg.a([128, 512], F32)
    T1 = asg.a([128, 512], F32)
    LNG = asg.a([128, 256], F32)
    LNB = asg.a([128, 256], F32)
    SGT = [asg.a([128, 512], BF16) for _ in range(2)]
    UT = [asg.a([128, 256], BF16) for _ in range(4)]
    VT = [asg.a([128, 256], F32) for _ in range(4)]
    VB = [asg.a([128, 256], BF16) for _ in range(4)]
    BST = asg.a([128, 16], F32)
    BSV = asg.a([128, 4, 2], F32)
    RSG = asg.a([128, 4], F32)
    aw = Arena(ARENA)
    GT1 = [aw.a([128, D], F32) for _ in range(2)]
    TMPW = [aw.a([128, D], F32) for _ in range(2)]
    ATN2 = [aw.a([128, D], F32) for _ in range(2)]
    BTN2 = [aw.a([128, D], F32) for _ in range(2)]
    GNB2 = aw.a([128, D], F32)
    aw2 = Arena(ATT_BASE)
    JUNK2 = aw2.a([128, D], BF16)
    XN2 = [aw2.a([128, D], BF16) for _ in range(2)]
    TMPN2 = [aw2.a([128, D], F32) for _ in range(2)]
    assert aw2.o <= ATT_BASE + NT * 768 * 2
    NSET2 = dict(JUNK=JUNK2, XN=XN2, TMPN=TMPN2, ATN=ATN2, BTN=BTN2, GNB=GNB2)
    af = Arena(ARENA)
    GT2 = [af.a([128, D], F32) for _ in range(2)]
    GBUF = [af.a([128, 2308], BF16) for _ in range(2)]
    TC = [af.a([128, 512], F32) for _ in range(3)]
    WDN = af.a([128, FG, D], BF16)
    TMPF = af.a([128, D], F32)
    ACTG, _ = sb([128, FG, TOK], BF16, at=ATT_BASE)
    afn = Arena(ARENA)
    FNGT = afn.a([128, D], F32)
    FJUNK = afn.a([128, D], BF16)
    OUTT = [afn.a([128, D], F32) for _ in range(2)]

    PSB = nc.alloc_psum_tensor("psb", [128, 3072], F32)
    PBB = nc.alloc_psum_tensor("pbb", [128, 2048], BF16)

    with ExitStack() as es:
        def sems(n, nm):
            return [es.enter_context(nc.semaphore("%s%d" % (nm, i))) for i in range(n)]

        PE = Eng("pe", nc.tensor, sems(12, "pe"))
        ACT = Eng("act", nc.scalar, sems(12, "ac"))
        DVE = Eng("dve", nc.vector, sems(16, "dv"))
        POOL = Eng("pool", nc.gpsimd, sems(2, "po"))
        SP = Eng("sp", nc.sync, sems(1, "sp"))
        POOL.dsems = sems(16, "pd")
        SP.dsems = sems(16, "sd")
        ALLE = [PE, ACT, DVE, POOL, SP]

        def bar():
            barrier(ALLE)

        Xt = [TT() for _ in range(NT)]
        HTt = [TT() for _ in range(NT)]
        ATTt = [TT() for _ in range(NT)]
        RGt = [TT() for _ in range(3)]
        PSt = [TT() for _ in range(6)]
        PBt = [TT() for _ in range(2)]
        ring_i = [0]
        ps1_i = [0]
        ps2_i = [0]
        pb_i = [0]

        def ring():
            i = ring_i[0] % 3
            ring_i[0] += 1
            return RINGS[i], RGt[i]

        def ps1():
            i = ps1_i[0] % 6
            ps1_i[0] += 1
            return PSB[:, i * 512:(i + 1) * 512], [PSt[i]]

        def ps2():
            i = ps2_i[0] % 3
            ps2_i[0] += 1
            return PSB[:, i * 1024:(i + 1) * 1024], [PSt[2 * i], PSt[2 * i + 1]]

        def pb():
            i = pb_i[0] % 2
            pb_i[0] += 1
            return PBB[:, i * 1024:(i + 1) * 1024], [PBt[i]]

        pre = {}

        def prefetch(key, src, ncols):
            slot, st = ring()
            dma(POOL, slot[:, 0:ncols], src, W=[st])
            pre[key] = (slot, st)

        def getw(key, src, ncols):
            if key in pre:
                return pre.pop(key)
            slot, st = ring()
            dma(POOL, slot[:, 0:ncols], src, W=[st])
            return slot, st

        def ht_tiles(t0, n):
            return HTt[t0 // 128:(t0 + n + 127) // 128]

        cst = TT()
        op(DVE, lambda: nc.vector.memset(ident[:], 1.0), W=[cst])
        op(POOL, lambda: nc.gpsimd.affine_select(out=ident[:], in_=ident[:], pattern=[[-1, 128]],
                                                  compare_op=ALU.is_equal, fill=0.0, base=0,
                                                  channel_multiplier=1), R=[cst], W=[cst])
        op(DVE, lambda: nc.vector.tensor_copy(identf[:], ident[:]), R=[cst], W=[cst])
        op(DVE, lambda: nc.vector.memset(blk64[:], 0.0), W=[cst])
        op(DVE, lambda: nc.vector.memset(blk64[0:64, 0:64], 1.0 / 64), W=[cst])
        op(DVE, lambda: nc.vector.memset(blk64[64:128, 64:128], 1.0 / 64), W=[cst])
        op(DVE, lambda: nc.vector.memset(ones_bf[:], 1.0), W=[cst])
        op(DVE, lambda: nc.vector.memset(epsc[:], EPS), W=[cst])
        sct = TT()
        dma(SP, SCf[:], cT, W=[sct])
        op(ACT, lambda: nc.scalar.activation(out=SC[:], in_=SCf[:], func=AF.Silu), R=[sct], W=[sct])

        def load_x(s):
            for t in range(NT):
                src = x2[s, t * 128:(t + 1) * 128, :] if t < 16 else ctx2[s, (t - 16) * 128:(t - 15) * 128, :]
                dma(SP, X[:, t, :], src, W=[Xt[t]])

        load_x(0)
        modt_l = [TT() for _ in range(nl)]

        def ada_piece(l, piece, slot, st, GRa, ABRa, gtile):
            w = piece // 2
            half = piece % 2
            sv = slot[:, 0:4096].rearrange("p (k n) -> p k n", n=512)
            pr, prt = ps1()
            for k in range(8):
                op(PE, lambda k=k: nc.tensor.matmul(pr[0:3, :], SC[:, k, :], sv[:, k, :],
                                                    start=(k == 0), stop=(k == 7)),
                   R=[st, sct], W=prt, inc=(k == 7))
            dma(SP, ABRa, adab[l, piece * 512:(piece + 1) * 512].partition_broadcast(3), W=[gtile])
            op(DVE, lambda: nc.vector.tensor_tensor(out=GRa, in0=pr[0:3, :], in1=ABRa, op=ALU.add),
               R=prt + [gtile], W=[gtile])
            dma(SP, grow[l, w, :, half * 512:(half + 1) * 512], GRa, R=[gtile], W=[modt_l[l]])

        for piece in range(12):
            slot, st = ring()
            dma(POOL, slot[:, 0:4096], adaw[0, piece], W=[st])
            gi = piece % 2
            ada_piece(0, piece, slot, st, GR[gi][0:3, :], ABR[gi][0:3, :], TT())
        bar()

        def load_params(l, vb):
            pt = TT()
            dma(SP, SGBC[:], sgbc[l], W=[pt])
            dma(SP, CVWC[:], cvwc[l], W=[pt])
            dma(SP, CVPC[:], cvpc[l], W=[pt])
            dma(SP, FCWC[:], fcwc[l], W=[pt])
            dma(SP, FCBC[:], fcbc[l], W=[pt])
            dma(POOL, SGW[:], sgwT[l], W=[pt])
            return pt

        def norm_setup(ntile, ni, l, s, NS):
            stt = dict(NS=NS, jt=TT(), sst=TT(), att=[TT(), TT()], xnt=[TT(), TT()], tnt=[TT(), TT()])
            gnt = TT()
            dma(SP, NS["GNB"][:], (n1g if ni == 0 else n2g)[l].partition_broadcast(128), W=[gnt])
            for vi, v in enumerate((s, 2)):
                if vi == 1 and ntile == 16:
                    continue
                dma(SP, NS["ATN"][vi][:], grow[l, 3 * ni + 1, v, :].partition_broadcast(128), R=[modt_l[l]], W=[stt["att"][vi]])
                dma(SP, NS["BTN"][vi][:], grow[l, 3 * ni, v, :].partition_broadcast(128), R=[modt_l[l]], W=[stt["att"][vi]])
                op(DVE, lambda vi=vi: nc.vector.scalar_tensor_tensor(out=NS["ATN"][vi][:], in0=NS["ATN"][vi][:], scalar=1.0,
                                                                     in1=NS["GNB"][:], op0=ALU.add, op1=ALU.mult),
                   R=[stt["att"][vi], gnt], W=[stt["att"][vi]])
            op(DVE, lambda: nc.vector.memset(SS[:], 0.0), W=[stt["sst"]])
            return stt

        def norm_tile_a(t, stt):
            NS = stt["NS"]
            sst = TT()
            stt.setdefault("sst_t", {})[t] = sst
            op(ACT, lambda: nc.scalar.activation(out=NS["JUNK"][:], in_=X[:, t, :], func=AF.Square,
                                                 accum_out=SS[:, 0, t:t + 1]), R=[Xt[t], stt["sst"]], W=[stt["jt"], sst])
            op(ACT, lambda: nc.scalar.activation(out=SS[:, 1, t:t + 1], in_=SS[:, 0, t:t + 1], func=AF.Sqrt,
                                                 bias=epsc[:], scale=1.0 / D), R=[sst, cst], W=[sst])

        def norm_tile_b1(t, stt):
            NS = stt["NS"]
            sst = stt["sst_t"][t]
            vi = 0 if t < 16 else 1
            xi = t % 2
            op(DVE, lambda: nc.vector.reciprocal(SS[:, 1, t:t + 1], SS[:, 1, t:t + 1]), R=[sst], W=[sst])
            op(DVE, lambda: nc.vector.scalar_tensor_tensor(
                out=NS["TMPN"][xi][:], in0=X[:, t, :], scalar=SS[:, 1, t:t + 1], in1=NS["ATN"][vi][:], op0=ALU.mult, op1=ALU.mult),
               R=[Xt[t], sst, stt["att"][vi]], W=[stt["tnt"][xi]])
            op(DVE, lambda: nc.vector.tensor_tensor(out=NS["XN"][xi][:], in0=NS["TMPN"][xi][:], in1=NS["BTN"][vi][:], op=ALU.add),
               R=[stt["tnt"][xi], stt["att"][vi]], W=[stt["xnt"][xi]])

        def norm_tile_b2(t, stt):
            NS = stt["NS"]
            xi = t % 2
            p, ptk = pb()
            for c in range(8):
                op(PE, lambda c=c: nc.tensor.transpose(p[:, c * 128:(c + 1) * 128],
                                                       NS["XN"][xi][:, c * 128:(c + 1) * 128], ident[:]),
                   R=[stt["xnt"][xi], cst], W=ptk, inc=(c == 7))
            dst = HT[:, :, t * 128:(t + 1) * 128]
            srcp = p[:, 0:1024].rearrange("p (c n) -> p c n", c=8)
            op(ACT, lambda: nc.scalar.copy(dst, srcp), R=ptk, W=[HTt[t]])

        def proj_fm(wv, c0, groups, evac):
            for (t0, n) in groups:
                p, ptk = ps1()
                for k in range(8):
                    op(PE, lambda k=k: nc.tensor.matmul(p[:, 0:n], wv[0][:, k, c0:c0 + 128], HT[:, k, t0:t0 + n],
                                                        start=(k == 0), stop=(k == 7)),
                       R=[wv[1]] + ht_tiles(t0, n), W=ptk, inc=(k == 7))
                evac(p, ptk, t0, n)

        for s in range(nseq):
            if s > 0:
                load_x(s)
            for l in range(nl):
                last = (l == nl - 1)
                ntile = 16 if last else NT
                groups_q = GROUPS[:4] if last else GROUPS
                pt = load_params(l, s)
                nst = norm_setup(NT, 0, l, s, NSET1)
                for t in range(NT + 2):
                    if t < NT:
                        norm_tile_a(t, nst)
                    if 0 <= t - 1 < NT:
                        norm_tile_b1(t - 1, nst)
                    if 0 <= t - 2 < NT:
                        norm_tile_b2(t - 2, nst)
                prefetch(("winA", 0), winA[l, 0], 3072)
                bar()
                qt_t = [TT() for _ in range(5)]
                kt_t = [TT() for _ in range(5)]
                vt_t = [TT() for _ in range(5)]
                va_t = [TT() for _ in range(5)]
                KG4 = KG[:, 0:32, :].rearrange("p (g w) (r c) -> p g w r c", w=4, c=32)
                VG4 = VTG[:, 0:32, :].rearrange("p (g w) (r c) -> p g w r c", w=4, c=32)
                for c in range(4):
                    slot, st = getw(("winA", c), winA[l, c], 3072)
                    wv = (slot[:, 0:3072].rearrange("p (k n) -> p k n", n=384), st)

                    def ev_q(p, ptk, t0, n):
                        gr = t0 // 512
                        if gr < 4:
                            dst = QG[:, 4 * gr:4 * gr + 4, :].rearrange("p b (r c) -> p b r c", c=16)
                            srcp = p[:, 0:512].rearrange("p (r b c) -> p b r c", b=4, c=16)
                        else:
                            dst = QG[:, 16:18, :]
                            srcp = p[:, 0:256].rearrange("p (b n) -> p b n", b=2)
                        op(ACT, lambda: nc.scalar.activation(out=dst, in_=srcp, func=AF.Copy, scale=0.125),
                           R=ptk, W=[qt_t[gr]])

                    def ev_gather(G4, GF, tl, eng0):
                        def ev(p, ptk, t0, n):
                            gr = t0 // 512
                            if gr < 4:
                                for w in range(4):
                                    dst = G4[:, 2 * gr:2 * gr + 2, w, :, :]
                                    srcp = p[:, 0:512].rearrange("p (g r c) -> p g r c", g=2, r=4)[
                                        :, :, :, AC_COL[w]:AC_COL[w] + 32]
                                    if (gr + eng0) % 2 == 0:
                                        op(DVE, lambda: nc.vector.tensor_copy(dst, srcp), R=ptk, W=[tl[gr]])
                                    else:
                                        op(ACT, lambda: nc.scalar.copy(dst, srcp), R=ptk, W=[tl[gr]])
                            else:
                                dst = GF[:, 32:34, :]
                                srcp = p[:, 0:256].rearrange("p (b n) -> p b n", b=2)
                                op(DVE, lambda: nc.vector.tensor_copy(dst, srcp), R=ptk, W=[tl[gr]])
                        return ev

                    proj_fm(wv, 0, groups_q, ev_q)
                    proj_fm(wv, 128, GROUPS, ev_gather(KG4, KG, kt_t, 0))
                    proj_fm(wv, 256, GROUPS, ev_gather(VG4, VTG, vt_t, 1))
                    if c == 0:
                        op(DVE, lambda: nc.vector.memset(VAUG[:, :, :, 64:65], 1.0), W=va_t)
                    for pk in range(5):
                        nk = 8 if pk < 4 else 2
                        p, ptk = pb()
                        for j in range(nk):
                            kt = pk * 8 + j
                            op(PE, lambda j=j, kt=kt: nc.tensor.transpose(p[:, j * 128:(j + 1) * 128], VTG[:, kt, :], ident[:]),
                               R=[vt_t[pk], cst], W=ptk, inc=(j == nk - 1))
                        dst = VAUG[:, pk * 8:pk * 8 + nk, :, 0:64]
                        srcp = p[:, 0:nk * 128].rearrange("p (a b c) -> p a b c", a=nk, b=2)
                        if pk % 2 == 0:
                            op(ACT, lambda dst=dst, srcp=srcp: nc.scalar.copy(dst, srcp), R=ptk, W=[va_t[pk]])
                        else:
                            op(DVE, lambda dst=dst, srcp=srcp: nc.vector.tensor_copy(dst, srcp), R=ptk, W=[va_t[pk]])
                    ptl = [TT(), TT()]
                    for hh in range(2):
                        h = 2 * c + hh
                        P0 = 64 * hh
                        bslot, bst = ring()
                        dma(POOL, bslot[:, 0:4608], btab[l, h], W=[bst])
                        BT = bslot[:, 0:4608].rearrange("p (t q) -> p t q", q=128)
                        nblk = 16 if last else 18
                        BTF = bslot[:, 0:4608]

                        def s_stage(qb):
                            di = qb % 2
                            p = PSB[:, di * 1024:(di + 1) * 1024]
                            ptk = [PSt[2 * di], PSt[2 * di + 1]]
                            pi = qb % 2
                            qa = QG[P0:P0 + 64, qb, :]
                            if qb < 16:
                                rb, cb = qb // 4, qb % 4
                                qtt = [qt_t[rb]]
                                g0 = AR_ROW[rb] // 4
                                ti0 = (PAT[rb] * 3 + PAT[cb]) * 4
                                op(PE, lambda: nc.tensor.matmul(p[:, 0:512], ident[:], BTF[:, ti0 * 128:ti0 * 128 + 512],
                                                                start=True, stop=False), R=[bst, cst], W=ptk, inc=False)
                                for j in range(4):
                                    g = g0 + j
                                    ka = KG[P0:P0 + 64, g * 4 + cb, :]
                                    op(PE, lambda j=j, ka=ka: nc.tensor.matmul(
                                        p[:, j * 128:(j + 1) * 128], ka, qa, start=False, stop=(j == 3),
                                        skip_group_check=True),
                                       R=[kt_t[g // 2]] + qtt, W=ptk, inc=False)
                                lo, nkt = 0, 6
                                kts = [(g0 + j) * 4 + cb for j in range(4)] + [32, 33]
                            else:
                                qtt = [qt_t[4]]
                                lo, nkt = 512, 2
                                kts = [32, 33]
                            for j in range(2):
                                op(PE, lambda j=j: nc.tensor.matmul(
                                    p[:, 512 + j * 128:512 + (j + 1) * 128],
                                    KG[P0:P0 + 64, 32 + j, :], qa, start=True, stop=True),
                                   R=[kt_t[4]] + qtt, W=ptk, inc=(j == 1))
                            op(ACT, lambda: nc.scalar.activation(
                                out=PT[pi][:, lo:lo + nkt * 128], in_=p[:, lo:lo + nkt * 128], func=AF.Exp),
                               R=ptk, W=[ptl[pi]])
                            return (qb, pi, lo, kts)

                        def pv_stage(st_):
                            qb, pi, lo, kts = st_
                            bi = 4 + qb % 2
                            po = PSB[:, bi * 512:(bi + 1) * 512]
                            pot = [PSt[bi]]
                            for ji, kt in enumerate(kts):
                                op(PE, lambda ji=ji, kt=kt: nc.tensor.matmul(
                                    po[:, 0:65], PT[pi][:, lo + ji * 128:lo + (ji + 1) * 128], VAUG[:, kt, hh, :],
                                    start=(ji == 0), stop=(ji == len(kts) - 1)),
                                   R=[ptl[pi], va_t[kt // 8]], W=pot, inc=(ji == len(kts) - 1))
                            rct = TT()
                            ri = qb % 8
                            op(DVE, lambda: nc.vector.reciprocal(RC[:, ri:ri + 1], po[:, 64:65]), R=pot, W=[rct])
                            op(DVE, lambda: nc.vector.tensor_scalar(
                                ATT[:, qb, h * 64:(h + 1) * 64], po[:, 0:64], RC[:, ri:ri + 1], None, ALU.mult),
                               R=pot + [rct], W=[ATTt[qb]])

                        prev = None
                        for qb in range(nblk):
                            cur = s_stage(qb)
                            if prev is not None:
                                pv_stage(prev)
                            prev = cur
                        pv_stage(prev)
                prefetch(("winB", 1), winB[l, 1], 4096)
                prefetch(("winB", 0), winB[l, 0], 4096)
                bar()
                cvt = TT()
                op(DVE, lambda: nc.vector.memset(CVH[:], 0.0), W=[cvt])
                slot, st = getw(("winB", 1), winB[l, 1], 4096)
                wv = (slot[:, 0:4096].rearrange("p (k n) -> p k n", n=512), st)
                groups_c = GROUPS[:4] if last else GROUPS
                sgtt = [TT(), TT()]
                for j in range(2):
                    for gi_, (t0, n) in enumerate(groups_c):
                        pa, pat = ps1()
                        pg, pgt = ps1()
                        for k in range(8):
                            op(PE, lambda k=k: nc.tensor.matmul(pa[:, 0:n], wv[0][:, k, j * 128:(j + 1) * 128],
                                                                HT[:, k, t0:t0 + n], start=(k == 0), stop=(k == 7)),
                               R=[st] + ht_tiles(t0, n), W=pat, inc=(k == 7))
                        for k in range(8):
                            op(PE, lambda k=k: nc.tensor.matmul(pg[:, 0:n], wv[0][:, k, 256 + j * 128:256 + (j + 1) * 128],
                                                                HT[:, k, t0:t0 + n], start=(k == 0), stop=(k == 7)),
                               R=[st] + ht_tiles(t0, n), W=pgt, inc=(k == 7))
                        si = gi_ % 2
                        op(ACT, lambda si=si: nc.scalar.activation(out=SGT[si][:, 0:n], in_=pg[:, 0:n], func=AF.Sigmoid),
                           R=pgt, W=[sgtt[si]])
                        base = 15 + t0 if t0 < SEQ else 2078 + 15 + (t0 - SEQ)
                        op(DVE, lambda si=si, base=base: nc.vector.tensor_tensor(
                            out=CVH[:, j, base:base + n], in0=pa[:, 0:n], in1=SGT[si][:, 0:n], op=ALU.mult),
                           R=pat + [sgtt[si]], W=[cvt])
                slot, st = getw(("winB", 0), winB[l, 0], 4096)
                wv = (slot[:, 0:4096].rearrange("p (k n) -> p k n", n=512), st)
                lnt = TT()
                dma(SP, LNG[:], sglng[l].partition_broadcast(128), W=[lnt])
                dma(SP, LNB[:], sglnb[l].partition_broadcast(128), W=[lnt])
                utt = [TT() for _ in range(4)]
                vtt = [TT() for _ in range(4)]
                vbt = [TT() for _ in range(4)]
                bsvt = [TT(), TT()]
                rsgt = [TT(), TT()]
                bstt = TT()

                def sg_a(t):
                    i4 = t % 4
                    p, ptk = ps1()
                    for k in range(8):
                        op(PE, lambda k=k: nc.tensor.matmul(p[:, 0:512], HT[:, k, t * 128:(t + 1) * 128], wv[0][:, k, :],
                                                            start=(k == 0), stop=(k == 7)),
                           R=[st, HTt[t]], W=ptk, inc=(k == 7))
                    op(ACT, lambda: nc.scalar.activation(out=UT[i4][:], in_=p[:, 0:256], func=AF.Gelu_apprx_tanh),
                       R=ptk, W=[utt[i4]])
                    op(ACT, lambda: nc.scalar.activation(out=VT[i4][:], in_=p[:, 256:512], func=AF.Gelu_apprx_tanh),
                       R=ptk, W=[vtt[i4]])
                    op(DVE, lambda: nc.vector.bn_stats(BST[:, 0:6], VT[i4][:]), R=[vtt[i4]], W=[bstt])
                    op(DVE, lambda: nc.vector.bn_aggr(BSV[:, i4, :], BST[:, 0:6]), R=[bstt], W=[bsvt[i4 // 2]])

                def sg_b(t0, n):
                    h2 = (t0 % 4) // 2
                    op(ACT, lambda: nc.scalar.activation(out=RSG[:, 2 * h2:2 * h2 + n], in_=BSV[:, 2 * h2:2 * h2 + n, 1], func=AF.Sqrt,
                                                         bias=epsc[:], scale=1.0), R=[bsvt[h2], cst], W=[rsgt[h2]])
                    op(DVE, lambda: nc.vector.reciprocal(RSG[:, 2 * h2:2 * h2 + n], RSG[:, 2 * h2:2 * h2 + n]), R=[rsgt[h2]], W=[rsgt[h2]])
                    for t in range(t0, t0 + n):
                        i4 = t % 4
                        op(DVE, lambda: nc.vector.scalar_tensor_tensor(out=VT[i4][:], in0=VT[i4][:], scalar=BSV[:, i4, 0:1], in1=LNG[:],
                                                                       op0=ALU.subtract, op1=ALU.mult),
                           R=[vtt[i4], bsvt[h2], lnt], W=[vtt[i4]])
                        op(DVE, lambda: nc.vector.scalar_tensor_tensor(out=VB[i4][:], in0=VT[i4][:], scalar=RSG[:, i4:i4 + 1], in1=LNB[:],
                                                                       op0=ALU.mult, op1=ALU.add),
                           R=[vtt[i4], rsgt[h2], lnt], W=[vbt[i4]])

                def sg_mix(t):
                    i2 = t % 4
                    pm, pmt = ps1()
                    for g in range(4):
                        op(PE, lambda g=g: nc.tensor.matmul(pm[:, g * 64:(g + 1) * 64], SGW[:, g * 128:(g + 1) * 128],
                                                            VB[i2][:, g * 64:(g + 1) * 64], start=True, stop=True),
                           R=[vbt[i2], pt], W=pmt, inc=(g == 3))
                    for g in range(4):
                        op(DVE, lambda g=g: nc.vector.scalar_tensor_tensor(
                            out=ATT[:, t, 512 + g * 64:512 + (g + 1) * 64], in0=pm[:, g * 64:(g + 1) * 64],
                            scalar=SGBC[:, g:g + 1], in1=UT[i2][:, g * 64:(g + 1) * 64], op0=ALU.add, op1=ALU.mult),
                           R=pmt + [utt[i2], pt], W=[ATTt[t]])

                dgt = TT()
                cvo_t = [TT(), TT()]
                sqt, mst, t1t = TT(), TT(), TT()
                cvi = [0]

                def cv_diag(j):
                    for k in range(31):
                        op(DVE, lambda k=k: nc.vector.tensor_scalar(DIAG31[:, k, :], ident[:], CVWC[:, j * 31 + k:j * 31 + k + 1],
                                                                    None, ALU.mult), R=[cst, pt], W=[dgt])

                def cv_conv(j, gi_):
                    t0, n = groups_c[gi_]
                    base = t0 if t0 < SEQ else 2078 + (t0 - SEQ)
                    pc, pct = ps1()
                    for k in range(31):
                        op(PE, lambda k=k: nc.tensor.matmul(pc[:, 0:n], DIAG31[:, k, :], CVH[:, j, base + k:base + k + n],
                                                            start=(k == 0), stop=(k == 30)),
                           R=[dgt, cvt], W=pct, inc=(k == 30))
                    ci = cvi[0] % 2
                    cvi[0] += 1
                    op(ACT, lambda: nc.scalar.activation(out=CVO[ci][:, 0:n], in_=pc[:, 0:n], func=AF.Identity,
                                                         bias=CVPC[:, j:j + 1], scale=1.0), R=pct + [pt], W=[cvo_t[ci]])
                    return (j, t0, n, ci)

                def cv_tail(h_):
                    j, t0, n, ci = h_
                    op(ACT, lambda: nc.scalar.activation(out=SQ[:, 0:n], in_=CVO[ci][:, 0:n], func=AF.Square),
                       R=[cvo_t[ci]], W=[sqt])
                    pme, pmet = ps1()
                    pq, pqt = ps1()
                    op(PE, lambda: nc.tensor.matmul(pme[:, 0:n], blk64[:], CVO[ci][:, 0:n], start=True, stop=True),
                       R=[cvo_t[ci], cst], W=pmet, inc=True)
                    op(PE, lambda: nc.tensor.matmul(pq[:, 0:n], blk64[:], SQ[:, 0:n], start=True, stop=True),
                       R=[sqt, cst], W=pqt, inc=True)
                    op(ACT, lambda: nc.scalar.copy(MS[:, 0:n], pme[:, 0:n]), R=pmet, W=[mst])
                    op(DVE, lambda: nc.vector.tensor_tensor(out=T1[:, 0:n], in0=MS[:, 0:n], in1=MS[:, 0:n], op=ALU.mult),
                       R=[mst], W=[t1t])
                    op(DVE, lambda: nc.vector.tensor_tensor(out=T1[:, 0:n], in0=pq[:, 0:n], in1=T1[:, 0:n], op=ALU.subtract),
                       R=pqt + [t1t], W=[t1t])
                    op(ACT, lambda: nc.scalar.activation(out=T1[:, 0:n], in_=T1[:, 0:n], func=AF.Sqrt, bias=epsc[:], scale=1.0),
                       R=[t1t, cst], W=[t1t])
                    op(DVE, lambda: nc.vector.tensor_tensor(out=CVO[ci][:, 0:n], in0=CVO[ci][:, 0:n], in1=MS[:, 0:n],
                                                            op=ALU.subtract), R=[cvo_t[ci], mst], W=[cvo_t[ci]])
                    op(DVE, lambda: nc.vector.reciprocal(T1[:, 0:n], T1[:, 0:n]), R=[t1t], W=[t1t])
                    op(DVE, lambda: nc.vector.tensor_tensor(out=CVO[ci][:, 0:n], in0=CVO[ci][:, 0:n], in1=T1[:, 0:n],
                                                            op=ALU.mult), R=[cvo_t[ci], t1t], W=[cvo_t[ci]])
                    op(ACT, lambda: nc.scalar.activation(out=HT[:, 6 + j, t0:t0 + n], in_=CVO[ci][:, 0:n], func=AF.Silu,
                                                         scale=CVPC[:, 2 + j:3 + j], bias=CVPC[:, 4 + j:5 + j]),
                       R=[cvo_t[ci], pt], W=ht_tiles(t0, n))

                cv_diag(0)
                pend_mix = []
                pend_cv = None
                pairs = [(t, min(2, ntile - t)) for t in range(0, ntile, 2)]
                for pi_, (t0, n) in enumerate(pairs):
                    for t in range(t0, t0 + n):
                        sg_a(t)
                    for t in pend_mix:
                        sg_mix(t)
                    sg_b(t0, n)
                    pend_mix = list(range(t0, t0 + n))
                    if pi_ % 2 == 1 and pi_ // 2 < len(groups_c):
                        cur = cv_conv(0, pi_ // 2)
                        if pend_cv is not None:
                            cv_tail(pend_cv)
                        pend_cv = cur
                for t in pend_mix:
                    sg_mix(t)
                if len(pairs) // 2 < len(groups_c):
                    cur = cv_conv(0, len(groups_c) - 1)
                    if pend_cv is not None:
                        cv_tail(pend_cv)
                    pend_cv = cur
                def tr_att(qb):
                    p, ptk = pb()
                    for c in range(4):
                        op(PE, lambda c=c: nc.tensor.transpose(p[:, c * 128:(c + 1) * 128], ATT[:, qb, c * 128:(c + 1) * 128],
                                                               ident[:]), R=[ATTt[qb], cst], W=ptk, inc=(c == 3))
                    if qb < 16:
                        rb, cb = qb // 4, qb % 4
                        dst = HT[:, 0:4, 0:SEQ].rearrange("p c (r w) -> p c r w", w=64)[:, :, 8 * rb:8 * rb + 8, 16 * cb:16 * cb + 16]
                        srcp = p[:, 0:512].rearrange("p (c r w) -> p c r w", c=4, r=8)
                        wt = HTt[4 * rb:4 * rb + 4]
                    else:
                        dst = HT[:, 0:4, SEQ + (qb - 16) * 128:SEQ + (qb - 15) * 128]
                        srcp = p[:, 0:512].rearrange("p (c n) -> p c n", c=4)
                        wt = [HTt[qb]]
                    if qb % 2 == 0:
                        op(ACT, lambda: nc.scalar.copy(dst, srcp), R=ptk, W=wt)
                    else:
                        op(DVE, lambda: nc.vector.tensor_copy(dst, srcp), R=ptk, W=wt)

                def tr_sg(t):
                    p, ptk = pb()
                    for c in range(2):
                        op(PE, lambda c=c: nc.tensor.transpose(p[:, c * 128:(c + 1) * 128],
                                                               ATT[:, t, 512 + c * 128:512 + (c + 1) * 128], ident[:]),
                           R=[ATTt[t], cst], W=ptk, inc=(c == 1))
                    dst = HT[:, 4:6, t * 128:(t + 1) * 128]
                    srcp = p[:, 0:256].rearrange("p (c n) -> p c n", c=2)
                    if t % 2 == 1:
                        op(ACT, lambda: nc.scalar.copy(dst, srcp), R=ptk, W=[HTt[t]])
                    else:
                        op(DVE, lambda: nc.vector.tensor_copy(dst, srcp), R=ptk, W=[HTt[t]])

                trs = [(tr_att, i) for i in range(ntile)] + [(tr_sg, i) for i in range(ntile)]
                per = (len(trs) + len(groups_c) - 1) // len(groups_c)
                cv_diag(1)
                for gi_ in range(len(groups_c)):
                    cur = cv_conv(1, gi_)
                    cv_tail(pend_cv)
                    pend_cv = cur
                    for _ in range(per):
                        if trs:
                            f_, i_ = trs.pop(0)
                            f_(i_)
                cv_tail(pend_cv)
                while trs:
                    f_, i_ = trs.pop(0)
                    f_(i_)
                prefetch(("wout", 0), woutd[l, 0], 4096)
                prefetch(("wout", 1), woutd[l, 1], 4096)
                bar()
                gtt = TT()
                dma(SP, GT1[0][:], grow[l, 2, s, :].partition_broadcast(128), R=[modt_l[l]], W=[gtt])
                dma(SP, GT1[1][:], grow[l, 2, 2, :].partition_broadcast(128), R=[modt_l[l]], W=[gtt])
                nst = norm_setup(ntile, 1, l, s, NSET2)
                wos = []
                for half in range(2):
                    slot, st = getw(("wout", half), woutd[l, half], 4096)
                    wos.append((slot[:, 0:4096].rearrange("p (k n) -> p k n", n=512), st))
                tmt = [TT(), TT()]
                for t in range(ntile + 3):
                    if t < ntile:
                        p, ptk = ps2()
                        for half in range(2):
                            for k in range(8):
                                op(PE, lambda k=k, half=half: nc.tensor.matmul(
                                    p[:, half * 512:(half + 1) * 512], HT[:, k, t * 128:(t + 1) * 128], wos[half][0][:, k, :],
                                    start=(k == 0), stop=(k == 7)),
                                   R=[wos[half][1], HTt[t]], W=ptk, inc=(k == 7 and half == 1))
                    if 0 <= t - 3 < ntile:
                        norm_tile_b2(t - 3, nst)
                    if t < ntile:
                        vi = 0 if t < 16 else 1
                        i2 = t % 2
                        op(DVE, lambda: nc.vector.tensor_tensor(out=TMPW[i2][:], in0=p[:, :], in1=GT1[vi][:], op=ALU.mult),
                           R=ptk + [gtt], W=[tmt[i2]])
                        op(DVE, lambda: nc.vector.tensor_tensor(out=X[:, t, :], in0=X[:, t, :], in1=TMPW[i2][:], op=ALU.add),
                           R=[tmt[i2], Xt[t]], W=[Xt[t]])
                        norm_tile_a(t, nst)
                    if 0 <= t - 1 < ntile:
                        norm_tile_b1(t - 1, nst)
                prefetch(("ffn", 0), wffn[l, 0, :, 0:2048], 2048)
                bar()
                gtt = TT()
                dma(SP, GT2[0][:], grow[l, 5, s, :].partition_broadcast(128), R=[modt_l[l]], W=[gtt])
                dma(SP, GT2[1][:], grow[l, 5, 2, :].partition_broadcast(128), R=[modt_l[l]], W=[gtt])
                gbt = [TT(), TT()]
                for i in range(2):
                    op(DVE, lambda i=i: nc.vector.memset(GBUF[i][:], 0.0), W=[gbt[i]])
                tct = [TT(), TT(), TT()]
                tcn = [0]
                wdt = [TT() for _ in range(FG)]
                agt = [TT() for _ in range(FG)]
                tft = TT()
                groups_f = GROUPS[:4] if last else GROUPS
                fgroups = [list(range(0, 6)), list(range(6, 12)), list(range(12, 17)), list(range(17, 22))]
                for fg in fgroups:
                    for fi, f in enumerate(fg):
                        slot, st = getw(("ffn", f), wffn[l, f, :, 0:2048], 2048)
                        if f + 1 < NF:
                            prefetch(("ffn", f + 1), wffn[l, f + 1, :, 0:2048], 2048)
                        dma(POOL, WDN[:, fi, :], wffn[l, f, :, 2048:3072], W=[wdt[fi]])
                        wv = (slot[:, 0:2048].rearrange("p (k n) -> p k n", n=256), st)
                        gi2 = f % 2
                        GB = GBUF[gi2]

                        def ev_g(p, ptk, t0, n):
                            base = 1 + t0 if t0 < SEQ else 2051 + (t0 - SEQ)
                            op(ACT, lambda: nc.scalar.copy(GB[:, base:base + n], p[:, 0:n]), R=ptk, W=[gbt[gi2]])

                        proj_fm(wv, 0, groups_f, ev_g)
                        if s == 0 and l + 1 < nl:
                            if 1 <= f <= 12:
                                aslot, ast = getw(("ada", f - 1), adaw[l + 1, f - 1], 4096)
                                ada_piece(l + 1, f - 1, aslot, ast, TMPF[0:3, 0:512], TMPF[0:3, 512:1024], tft)
                        w0c = FCWC[:, f * 3:f * 3 + 1]
                        w1c = FCWC[:, f * 3 + 1:f * 3 + 2]
                        w2c = FCWC[:, f * 3 + 2:f * 3 + 3]

                        def fin(pd):
                            ti, pu, put, t0, n = pd
                            op(ACT, lambda: nc.scalar.activation(out=TC[ti][:, 0:n], in_=TC[ti][:, 0:n], func=AF.Silu),
                               R=[tct[ti]], W=[tct[ti]])
                            op(DVE, lambda: nc.vector.tensor_tensor(out=ACTG[:, fi, t0:t0 + n], in0=pu[:, 0:n],
                                                                    in1=TC[ti][:, 0:n], op=ALU.mult),
                               R=put + [tct[ti]], W=[agt[fi]])

                        pend = None
                        for gi_, (t0, n) in enumerate(groups_f):
                            base = t0 if t0 < SEQ else 2050 + (t0 - SEQ)
                            ti = tcn[0] % 3
                            tcn[0] += 1
                            op(ACT, lambda: nc.scalar.activation(out=TC[ti][:, 0:n], in_=GB[:, base + 1:base + 1 + n], func=AF.Identity,
                                                                 scale=w1c, bias=FCBC[:, f:f + 1]), R=[gbt[gi2], pt], W=[tct[ti]])
                            op(DVE, lambda: nc.vector.scalar_tensor_tensor(out=TC[ti][:, 0:n], in0=GB[:, base:base + n], scalar=w0c,
                                                                            in1=TC[ti][:, 0:n], op0=ALU.mult, op1=ALU.add),
                               R=[gbt[gi2], pt, tct[ti]], W=[tct[ti]])
                            op(DVE, lambda: nc.vector.scalar_tensor_tensor(out=TC[ti][:, 0:n], in0=GB[:, base + 2:base + 2 + n], scalar=w2c,
                                                                            in1=TC[ti][:, 0:n], op0=ALU.mult, op1=ALU.add),
                               R=[gbt[gi2], pt, tct[ti]], W=[tct[ti]])
                            pu, put = ps1()
                            for k in range(8):
                                op(PE, lambda k=k: nc.tensor.matmul(pu[:, 0:n], wv[0][:, k, 128:256], HT[:, k, t0:t0 + n],
                                                                    start=(k == 0), stop=(k == 7)),
                                   R=[st] + ht_tiles(t0, n), W=put, inc=(k == 7))
                            if pend is not None:
                                fin(pend)
                            pend = (ti, pu, put, t0, n)
                        fin(pend)
                        if s == 0 and l + 1 < nl and f <= 11:
                            prefetch(("ada", f), adaw[l + 1, f], 4096)
                    ng = len(fg)
                    for t in range(ntile):
                        p, ptk = ps2()
                        for half in range(2):
                            for fi in range(ng):
                                op(PE, lambda fi=fi, half=half: nc.tensor.matmul(
                                    p[:, half * 512:(half + 1) * 512], ACTG[:, fi, t * 128:(t + 1) * 128],
                                    WDN[:, fi, half * 512:(half + 1) * 512], start=(fi == 0), stop=(fi == ng - 1)),
                                   R=[agt[fi], wdt[fi]], W=ptk, inc=(fi == ng - 1 and half == 1))
                        vi = 0 if t < 16 else 1
                        op(DVE, lambda vi=vi: nc.vector.tensor_tensor(out=TMPF[:], in0=p[:, :], in1=GT2[vi][:], op=ALU.mult),
                           R=ptk + [gtt], W=[tft])
                        op(DVE, lambda t=t: nc.vector.tensor_tensor(out=X[:, t, :], in0=X[:, t, :], in1=TMPF[:], op=ALU.add),
                           R=[tft, Xt[t]], W=[Xt[t]])
                bar()
            fgt = TT()
            dma(SP, FNGT[:], fng.partition_broadcast(128), W=[fgt])
            jt, sst = TT(), TT()
            op(DVE, lambda: nc.vector.memset(SS[:], 0.0), W=[sst])
            for t in range(16):
                op(ACT, lambda t=t: nc.scalar.activation(out=FJUNK[:], in_=X[:, t, :], func=AF.Square,
                                                         accum_out=SS[:, 0, t:t + 1]), R=[Xt[t]], W=[jt, sst])
            op(ACT, lambda: nc.scalar.activation(out=SS[:, 1, 0:16], in_=SS[:, 0, 0:16], func=AF.Sqrt, bias=epsc[:],
                                                 scale=1.0 / D), R=[sst, cst], W=[sst])
            op(DVE, lambda: nc.vector.reciprocal(SS[:, 1, 0:16], SS[:, 1, 0:16]), R=[sst], W=[sst])
            ott = [TT(), TT()]
            for t in range(16):
                i2 = t % 2
                op(DVE, lambda t=t, i2=i2: nc.vector.scalar_tensor_tensor(out=OUTT[i2][:], in0=X[:, t, :], scalar=SS[:, 1, t:t + 1],
                                                                          in1=FNGT[:], op0=ALU.mult, op1=ALU.mult),
                   R=[Xt[t], sst, fgt], W=[ott[i2]])
                dma(SP, y2[s, t * 128:(t + 1) * 128, :], OUTT[i2][:], R=[ott[i2]])
            bar()
        k = SP.dcount
        ns = len(SP.dsems)
        for kk in range(max(0, k - ns), k):
            need(SP, ("d", SP.dsems[kk % ns], 16 * (kk // ns + 1)))
    return nc


def _btab(rpb):
    nlay = rpb.shape[0]
    tab = np.empty((nlay, 8, 128, 36, 128), np.float32)
    RB = [0, 1, 3]
    for rp in range(3):
        rb = RB[rp]
        for j in range(4):
            kr = AR_ROW[rb] + 4 * j + np.arange(4)
            qr = 8 * rb + np.arange(8)
            sr = np.clip(qr - 4, 0, 24)
            mr = (kr[:, None] >= sr[None, :]) & (kr[:, None] < sr[None, :] + 8)
            ri = np.clip(kr[:, None] - qr[None, :] + 7, 0, 14)
            for cp in range(3):
                cb = RB[cp]
                kc = AC_COL[cb] + np.arange(32)
                qc = 16 * cb + np.arange(16)
                sc = np.clip(qc - 8, 0, 48)
                mc = (kc[:, None] >= sc[None, :]) & (kc[:, None] < sc[None, :] + 16)
                ci = np.clip(kc[:, None] - qc[None, :] + 15, 0, 30)
                R = np.broadcast_to(ri[:, None, :, None], (4, 32, 8, 16))
                C = np.broadcast_to(ci[None, :, None, :], (4, 32, 8, 16))
                vals = rpb[:, :, R, C]
                m = mr[:, None, :, None] & mc[None, :, None, :]
                vals = np.where(m[None, None], vals, np.float32(-30000.0))
                ti = (rp * 3 + cp) * 4 + j
                tab[:, :, :, ti, :] = vals.reshape(nlay, 8, 128, 128)
    return tab.reshape(nlay, 8, 128, 4608)


def _prep_shared(i, nl=NL):
    f = np.float32
    c_ = np.ascontiguousarray
    sh = {}
    aw = i["ada_w"][:nl].reshape(nl, 8, 128, 12, 512)
    sh["adaw"] = c_(aw.transpose(0, 3, 2, 1, 4)).reshape(nl, 12, 128, 4096)
    sh["adab"] = c_(i["ada_b"][:nl])
    sh["n1g"] = c_(i["norm1_g"][:nl])
    sh["n2g"] = c_(i["norm2_g"][:nl])
    sh["fng"] = c_(i["final_norm_g"])
    wi = i["w_in"][:nl].reshape(nl, 8, 128, 2560)
    pa = []
    for c in range(4):
        cols = np.concatenate([np.arange(128 * c, 128 * c + 128), 512 + np.arange(128 * c, 128 * c + 128),
                               1024 + np.arange(128 * c, 128 * c + 128)])
        pa.append(wi[:, :, :, cols].transpose(0, 2, 1, 3).reshape(nl, 128, 3072))
    sh["winA"] = c_(np.stack(pa, 1))
    pb_ = [wi[:, :, :, 1536:2048].transpose(0, 2, 1, 3).reshape(nl, 128, 4096),
           wi[:, :, :, 2048:2560].transpose(0, 2, 1, 3).reshape(nl, 128, 4096)]
    sh["winB"] = c_(np.stack(pb_, 1))
    sh["btab"] = _btab(i["na_rpb"][:nl])
    sh["sglng"] = c_(i["sg_norm_g"][:nl])
    sh["sglnb"] = c_(i["sg_norm_b"][:nl])
    sh["sgwT"] = c_(i["sg_w"][:nl].transpose(0, 3, 1, 2)).reshape(nl, 128, 512)
    sh["sgbc"] = c_(i["sg_b"][:nl].transpose(0, 2, 1))
    sh["cvwc"] = c_(i["cv_w"][:nl].reshape(nl, 31, 2, 128).transpose(0, 3, 2, 1)).reshape(nl, 128, 62)
    cvp = np.stack([i["cv_b"][:nl], i["cv_norm_g"][:nl], i["cv_norm_b"][:nl]], 1)
    sh["cvpc"] = c_(cvp.reshape(nl, 3, 2, 128).transpose(0, 3, 1, 2)).reshape(nl, 128, 6)
    wo = i["w_out"][:nl].reshape(nl, 8, 128, 2, 512)
    sh["woutd"] = c_(wo.transpose(0, 3, 2, 1, 4)).reshape(nl, 2, 128, 4096)
    up = i["ffn_w_up"][:nl].reshape(nl, 8, 128, 2, NF, 128)
    upf = up.transpose(0, 4, 2, 1, 3, 5).reshape(nl, NF, 128, 2048)
    dn = i["ffn_w_down"][:nl].reshape(nl, NF, 128, 1024)
    sh["wffn"] = c_(np.concatenate([upf, dn], axis=3))
    sh["fcwc"] = c_(i["ffn_conv_w"][:nl].reshape(nl, 3, NF, 128).transpose(0, 3, 2, 1)).reshape(nl, 128, 66)
    sh["fcbc"] = c_(i["ffn_conv_b"][:nl].reshape(nl, NF, 128).transpose(0, 2, 1))
    return {k: np.asarray(v, f) for k, v in sh.items()}


def _run(inputs, nl=NL, nseq=2, ncores=8):
    i = {k: np.asarray(v) for k, v in inputs.items()}
    sh = _prep_shared(i, nl)
    nc = build(nl, nseq)
    in_maps = []
    for core in range(ncores):
        b0 = 2 * core
        m = dict(sh)
        m["x2"] = np.ascontiguousarray(i["x"][b0:b0 + 2])
        m["ctx2"] = np.ascontiguousarray(i["ctx"][b0:b0 + 2])
        c3 = np.stack([i["c"][b0], i["c"][b0 + 1], i["c_ctx"]], 0)
        m["cT"] = np.ascontiguousarray(c3.reshape(3, 8, 128).transpose(2, 1, 0)).astype(np.float32)
        in_maps.append(m)
    res = run_bass_kernel_spmd(nc, in_maps, core_ids=list(range(ncores)))
    out = np.concatenate([r["y2"] for r in res.results], axis=0)
    return out.astype(np.float32)


def kernel(**inputs):
    return _run(inputs)
```
